# Optimizing a Trainium2 kernel written in Bass

```python
import math
import jax, jax.numpy as jnp
from jax import lax
import numpy as np

D_MODEL = 1024
BATCH = 8
SEQ = 4096
DEPTH = 4

CHUNK = 64
QBLK = 128
HEAD_DIM = 64
H_SB = 4
H_FOX = 4
H_DIFF = 4
W_SB = H_SB * HEAD_DIM
W_FOX = H_FOX * HEAD_DIM
W_DIFF_QK = H_DIFF * 2 * HEAD_DIM
DIFF_V_DIM = 2 * HEAD_DIM
W_DIFF = H_DIFF * DIFF_V_DIM
MIX_WIDTH = W_SB + W_FOX + W_DIFF
N_BRANCH = 3
ROT_DIM = HEAD_DIM // 4
ROPE_THETA = 500000.0
D_FF = 2816
P_DIM = 256
EPS = 1e-6
FORGET_BIAS_INIT = 2.0
IN_SIZES = (W_SB, W_SB, W_SB, W_FOX, W_FOX, W_FOX, H_FOX, W_DIFF_QK, W_DIFF_QK, W_DIFF, N_BRANCH * D_MODEL)
IN_COLS = 3 * W_SB + 3 * W_FOX + H_FOX + 2 * W_DIFF_QK + W_DIFF + N_BRANCH * D_MODEL

kernel_name = 'hybrid_sb_fox_diff_macaron_ple'


def rms_norm(x, gain):
    xf = x.astype(jnp.float32)
    y = xf * lax.rsqrt(jnp.mean(xf * xf, axis=-1, keepdims=True) + EPS)
    return (y * gain.astype(jnp.float32)).astype(x.dtype)


def swiglu_ffn(h, gain, wi, wo):
    a, g = jnp.split(rms_norm(h, gain) @ wi, 2, axis=-1)
    return (jax.nn.silu(a) * g) @ wo


def rope_tables(positions):
    inv_freq = ROPE_THETA ** (-jnp.arange(0, ROT_DIM, 2, dtype=jnp.float32) / ROT_DIM)
    ang = positions.astype(jnp.float32)[..., None] * inv_freq
    return jnp.cos(ang), jnp.sin(ang)


def apply_partial_rope(x, cos, sin):
    c = cos[:, :, None, None, :]
    s = sin[:, :, None, None, :]
    x1 = x[..., :ROT_DIM // 2].astype(jnp.float32)
    x2 = x[..., ROT_DIM // 2:ROT_DIM].astype(jnp.float32)
    rot = jnp.concatenate([x1 * c - x2 * s, x2 * c + x1 * s], axis=-1).astype(x.dtype)
    return jnp.concatenate([rot, x[..., ROT_DIM:]], axis=-1)


def to_heads(x, n_heads, d):
    b, s, _ = x.shape
    return x.reshape(b, s, n_heads, d).transpose(0, 2, 1, 3)


def from_heads(o):
    b, h, s, d = o.shape
    return o.transpose(0, 2, 1, 3).reshape(b, s, h * d)


def block_indices(i):
    lo, hi = i * QBLK, (i + 1) * QBLK
    t_idx = lo + jnp.arange(QBLK)[:, None]
    s_idx = jnp.arange(hi)[None, :]
    return lo, hi, t_idx, s_idx


def stick_breaking_attention(q, k, v):
    scale = HEAD_DIM ** -0.5
    outs = []
    for i in range(q.shape[2] // QBLK):
        lo, hi, t_idx, s_idx = block_indices(i)
        strict = s_idx < t_idx
        z = jnp.einsum('bhqd,bhkd->bhqk', q[:, :, lo:hi], k[:, :, :hi]).astype(jnp.float32) * scale
        log_beta = jax.nn.log_sigmoid(z)
        log_keep = jnp.where(strict, jax.nn.log_sigmoid(-z), 0.0)
        tail = lax.cumsum(log_keep, axis=3, reverse=True) - log_keep
        a = jnp.where(strict, jnp.exp(log_beta + tail), 0.0)
        outs.append(jnp.einsum('bhqk,bhkd->bhqd', a.astype(v.dtype), v[:, :, :hi]))
    return jnp.concatenate(outs, axis=2)


def forgetting_attention(q, k, v, log_f):
    scale = HEAD_DIM ** -0.5
    cum = jnp.cumsum(log_f, axis=-1)
    outs = []
    for i in range(q.shape[2] // QBLK):
        lo, hi, t_idx, s_idx = block_indices(i)
        logits = jnp.einsum('bhqd,bhkd->bhqk', q[:, :, lo:hi], k[:, :, :hi]).astype(jnp.float32) * scale
        logits = logits + cum[:, :, lo:hi, None] - cum[:, :, None, :hi]
        probs = jax.nn.softmax(jnp.where(s_idx <= t_idx, logits, -jnp.inf), axis=-1)
        outs.append(jnp.einsum('bhqk,bhkd->bhqd', probs.astype(v.dtype), v[:, :, :hi]))
    return jnp.concatenate(outs, axis=2)


def differential_attention(q1, q2, k1, k2, v, lam):
    scale = HEAD_DIM ** -0.5
    outs = []
    for i in range(q1.shape[2] // QBLK):
        lo, hi, t_idx, s_idx = block_indices(i)
        mask = (s_idx // CHUNK) <= (t_idx // CHUNK)
        s1 = jnp.einsum('bhqd,bhkd->bhqk', q1[:, :, lo:hi], k1[:, :, :hi]).astype(jnp.float32) * scale
        s2 = jnp.einsum('bhqd,bhkd->bhqk', q2[:, :, lo:hi], k2[:, :, :hi]).astype(jnp.float32) * scale
        a1 = jax.nn.softmax(jnp.where(mask, s1, -jnp.inf), axis=-1)
        a2 = jax.nn.softmax(jnp.where(mask, s2, -jnp.inf), axis=-1)
        w = a1 - lam * a2
        outs.append(jnp.einsum('bhqk,bhkd->bhqd', w.astype(v.dtype), v[:, :, :hi]))
    return jnp.concatenate(outs, axis=2)


def token_mixing(u, cos, sin, w_in, b_forget, qk_gain_fox, qk_gain_diff, diff_lambda, diff_subln, w_br, w_o, lam_init):
    b, s, _ = u.shape
    split_idx = np.cumsum(IN_SIZES)[:-1].tolist()
    qa, ka, va, qb, kb, vb, fb, qc, kc, vc, gates = jnp.split(u @ w_in, split_idx, axis=-1)

    oa = stick_breaking_attention(to_heads(qa, H_SB, HEAD_DIM), to_heads(ka, H_SB, HEAD_DIM), to_heads(va, H_SB, HEAD_DIM))

    qb = rms_norm(qb.reshape(b, s, H_FOX, HEAD_DIM), qk_gain_fox[0]).transpose(0, 2, 1, 3)
    kb = rms_norm(kb.reshape(b, s, H_FOX, HEAD_DIM), qk_gain_fox[1]).transpose(0, 2, 1, 3)
    log_f = jax.nn.log_sigmoid(fb.astype(jnp.float32) + b_forget.astype(jnp.float32)).transpose(0, 2, 1)
    ob = forgetting_attention(qb, kb, to_heads(vb, H_FOX, HEAD_DIM), log_f)

    qc = apply_partial_rope(rms_norm(qc.reshape(b, s, H_DIFF, 2, HEAD_DIM), qk_gain_diff[0]), cos, sin)
    kc = apply_partial_rope(rms_norm(kc.reshape(b, s, H_DIFF, 2, HEAD_DIM), qk_gain_diff[1]), cos, sin)
    q1, q2 = qc[:, :, :, 0].transpose(0, 2, 1, 3), qc[:, :, :, 1].transpose(0, 2, 1, 3)
    k1, k2 = kc[:, :, :, 0].transpose(0, 2, 1, 3), kc[:, :, :, 1].transpose(0, 2, 1, 3)
    lf = diff_lambda.astype(jnp.float32)
    lam = jnp.exp(jnp.sum(lf[0] * lf[1])) - jnp.exp(jnp.sum(lf[2] * lf[3])) + lam_init
    oc = differential_attention(q1, q2, k1, k2, to_heads(vc, H_DIFF, DIFF_V_DIM), lam)
    oc = rms_norm(oc, diff_subln) * (1.0 - lam_init)

    y_a = from_heads(oa) @ w_br[:W_SB]
    y_b = from_heads(ob) @ w_br[W_SB:W_SB + W_FOX]
    y_c = from_heads(oc) @ w_br[W_SB + W_FOX:]
    g = jax.nn.sigmoid(gates.reshape(b, s, N_BRANCH, D_MODEL))
    merged = g[:, :, 0] * y_a + g[:, :, 1] * y_b + g[:, :, 2] * y_c
    return merged @ w_o


def setup_inputs(seed: int = 0) -> dict:
    key = jax.random.key(seed)
    ks = jax.random.split(key, 24)

    def w(k, shape, fan_in, gain=1.0):
        return jax.random.normal(k, shape, jnp.float32) * (gain * fan_in ** -0.5)

    def gn(k, shape):
        return 1.0 + 0.05 * jax.random.normal(k, shape, jnp.float32)

    x = jax.random.normal(ks[0], (BATCH, SEQ, D_MODEL), jnp.float32)
    p = jax.random.normal(ks[1], (DEPTH, BATCH, SEQ, P_DIM), jnp.float32)
    start = jax.random.randint(ks[2], (BATCH, 1), 0, 8192, dtype=jnp.int32)
    positions = start + jnp.arange(SEQ, dtype=jnp.int32)[None, :]
    return {
        'x': x,
        'p': p,
        'positions': positions,
        'ffn1_norm': gn(ks[3], (DEPTH, D_MODEL)),
        'ffn1_wi': w(ks[4], (DEPTH, D_MODEL, 2 * D_FF), D_MODEL),
        'ffn1_wo': w(ks[5], (DEPTH, D_FF, D_MODEL), D_FF, 0.5),
        'mix_norm': gn(ks[6], (DEPTH, D_MODEL)),
        'w_in': w(ks[7], (DEPTH, D_MODEL, IN_COLS), D_MODEL),
        'b_forget': FORGET_BIAS_INIT + 0.1 * jax.random.normal(ks[8], (DEPTH, H_FOX), jnp.float32),
        'qk_gain_fox': gn(ks[9], (DEPTH, 2, HEAD_DIM)),
        'qk_gain_diff': gn(ks[10], (DEPTH, 2, HEAD_DIM)),
        'diff_lambda': 0.1 * jax.random.normal(ks[11], (DEPTH, 4, HEAD_DIM), jnp.float32),
        'diff_subln': gn(ks[12], (DEPTH, DIFF_V_DIM)),
        'w_br': w(ks[13], (DEPTH, MIX_WIDTH, D_MODEL), MIX_WIDTH),
        'w_o': w(ks[14], (DEPTH, D_MODEL, D_MODEL), D_MODEL, 0.5),
        'ffn2_norm': gn(ks[15], (DEPTH, D_MODEL)),
        'ffn2_wi': w(ks[16], (DEPTH, D_MODEL, 2 * D_FF), D_MODEL),
        'ffn2_wo': w(ks[17], (DEPTH, D_FF, D_MODEL), D_FF, 0.5),
        'ple_norm': gn(ks[18], (DEPTH, D_MODEL)),
        'ple_gate_w': w(ks[19], (DEPTH, D_MODEL, D_MODEL), D_MODEL),
        'ple_proj_w': w(ks[20], (DEPTH, P_DIM, D_MODEL), P_DIM, 0.5),
    }


def reference(x, p, positions, ffn1_norm, ffn1_wi, ffn1_wo, mix_norm, w_in, b_forget, qk_gain_fox, qk_gain_diff, diff_lambda, diff_subln, w_br, w_o, ffn2_norm, ffn2_wi, ffn2_wo, ple_norm, ple_gate_w, ple_proj_w):
    cos, sin = rope_tables(positions)
    h = x
    for i in range(DEPTH):
        lam_init = 0.8 - 0.6 * math.exp(-0.3 * i)
        h = h + 0.5 * swiglu_ffn(h, ffn1_norm[i], ffn1_wi[i], ffn1_wo[i])
        u = rms_norm(h, mix_norm[i])
        h = h + token_mixing(u, cos, sin, w_in[i], b_forget[i], qk_gain_fox[i], qk_gain_diff[i], diff_lambda[i], diff_subln[i], w_br[i], w_o[i], lam_init)
        h = h + 0.5 * swiglu_ffn(h, ffn2_norm[i], ffn2_wi[i], ffn2_wo[i])
        gate = jax.nn.sigmoid(rms_norm(h, ple_norm[i]) @ ple_gate_w[i])
        h = h + gate * (p[i] @ ple_proj_w[i])
    return h
```

```python
import math
from contextlib import ExitStack

import numpy as np
import concourse.bass as bass
import concourse.mybir as mybir
from concourse.bass_utils import run_bass_kernel_spmd

F32 = mybir.dt.float32
BF16 = mybir.dt.bfloat16
I32 = mybir.dt.int32
AF = mybir.ActivationFunctionType
ALU = mybir.AluOpType

D = 1024
DFF = 2816
NFC = DFF // 128
PD = 256
EPS = 1e-6
NEG = -30000.0
NCOLS = 40
TT = 512


class Prog:
    ENGS = ("pe", "act", "dve", "pool", "sp")

    def __init__(self, nc, stack):
        self.nc = nc
        self.stack = stack
        self.ops = {e: [] for e in self.ENGS}
        self.sem = {e: stack.enter_context(nc.semaphore("s_" + e)) for e in self.ENGS}
        self.cnt = {e: 0 for e in self.ENGS}
        self.known = {e: {} for e in self.ENGS}
        self.lastw = {}
        self.readers = {}
        self.pending = {e: [] for e in self.ENGS}
        self.dsems = {}
        self.n_ops = 0

    def dsem(self, key):
        d = self.dsems.get(key)
        if d is None:
            s = self.stack.enter_context(self.nc.semaphore("d%d" % len(self.dsems)))
            d = {"sem": s, "cnt": 0}
            self.dsems[key] = d
        return d

    def _need(self, eng, tok, waits):
        if tok is None:
            return
        sem, val, src = tok
        if src == eng and eng == "pe":
            return
        k = self.known[eng]
        if k.get(id(sem), 0) >= val:
            return
        k[id(sem)] = val
        waits.append((sem, val))

    def _deps(self, eng, reads, writes):
        waits = []
        for r in reads:
            self._need(eng, self.lastw.get(r), waits)
        for w in writes:
            self._need(eng, self.lastw.get(w), waits)
            for t in self.readers.get(w, ()):
                self._need(eng, t, waits)
        best = {}
        order = []
        for sem, val in waits:
            if id(sem) not in best:
                order.append(sem)
                best[id(sem)] = val
            else:
                best[id(sem)] = max(best[id(sem)], val)
        return [(s, best[id(s)]) for s in order]

    def _commit(self, tok, reads, writes):
        for w in writes:
            self.lastw[w] = tok
            self.readers[w] = []
        for r in reads:
            self.readers.setdefault(r, []).append(tok)

    def op(self, eng, fn, reads=(), writes=(), inc=True):
        waits = self._deps(eng, reads, writes)
        self.n_ops += 1 + len(waits)
        if inc:
            self.cnt[eng] += 1
            tok = (self.sem[eng], self.cnt[eng], eng)
            self.ops[eng].append((waits, fn, (self.sem[eng], 1)))
            for (r, w) in self.pending[eng]:
                self._commit(tok, r, w)
            self.pending[eng] = []
            self._commit(tok, reads, writes)
        else:
            self.ops[eng].append((waits, fn, None))
            self.pending[eng].append((tuple(reads), tuple(writes)))

    def dma(self, eng, skey, out, in_, reads=(), writes=()):
        self.dma_many(eng, skey, [(out, in_)], reads, writes)

    def dma_many(self, eng, skey, pairs, reads=(), writes=()):
        d = self.dsem(skey)
        waits = self._deps(eng, reads, writes)
        self.n_ops += len(pairs) + len(waits)
        first = True
        for (o, i) in pairs:
            d["cnt"] += 16
            self.ops[eng].append((waits if first else [], (lambda e, o=o, i=i: e.dma_start(out=o, in_=i)), (d["sem"], 16)))
            first = False
        tok = (d["sem"], d["cnt"], "dma")
        self._commit(tok, reads, writes)

    def barrier(self):
        assert all(not p for p in self.pending.values())
        for e in self.ENGS:
            waits = []
            for o in self.ENGS:
                if o != e and self.cnt[o] > 0:
                    self._need(e, (self.sem[o], self.cnt[o], o), waits)
            for d in self.dsems.values():
                if d["cnt"] > 0:
                    self._need(e, (d["sem"], d["cnt"], "dma"), waits)
            self.ops[e].append((waits, None, None))
        self.lastw = {}
        self.readers = {}

    def emit(self):
        nc = self.nc
        with nc.Block() as block:
            def run(name):
                def body(e):
                    for waits, fn, inc in self.ops[name]:
                        for sem, val in waits:
                            e.wait_ge(sem, val)
                        if fn is not None:
                            ins = fn(e)
                            if inc is not None:
                                ins.then_inc(inc[0], inc[1])
                return body
            block.tensor(run("pe"))
            block.scalar(run("act"))
            block.vector(run("dve"))
            block.gpsimd(run("pool"))
            block.sync(run("sp"))


class Builder:
    def __init__(self, S, layers, n_layers_total, lam_inits, stages=None, debug=False):
        self.S = S
        self.NT = S // TT
        self.NB = S // 128
        self.layers = layers
        self.stages = stages
        self.debug = debug
        self.lam_inits = lam_inits
        L = n_layers_total
        nc = bass.Bass("TRN2", target_bir_lowering=False)
        self.nc = nc
        dt = nc.dram_tensor
        self.xT = dt("xT", [D, S], F32, kind="ExternalInput").ap()
        self.pT = dt("pT", [L, PD, S], F32, kind="ExternalInput").ap()
        self.posb = dt("posb", [128, S], I32, kind="ExternalInput").ap()
        self.w = {}
        for nm, shp in [("ffn1_wi", [L, D, 2 * DFF]), ("ffn1_wo", [L, DFF, D]), ("w_in", [L, D, 6148]),
                        ("w_br", [L, D, D]), ("w_o", [L, D, D]), ("ffn2_wi", [L, D, 2 * DFF]),
                        ("ffn2_wo", [L, DFF, D]), ("ple_gate_w", [L, D, D]), ("ple_proj_w", [L, PD, D])]:
            self.w[nm] = dt(nm, shp, F32, kind="ExternalInput").ap()
        self.cols_d = dt("cols", [L, 128, NCOLS], F32, kind="ExternalInput").ap()
        self.dlam_d = dt("dlam", [L, 128, 256], F32, kind="ExternalInput").ap()
        self.cbf_d = dt("cbf", [128, 5 * 128 + 12 * 512], F32, kind="ExternalInput").ap()
        self.c32_d = dt("c32", [128, 2 * 128 + 3], F32, kind="ExternalInput").ap()
        self.yT = dt("yT", [D, S], F32, kind="ExternalOutput").ap()
        kd = "ExternalOutput" if debug else "Internal"
        self.hA = dt("hA", [D, S], F32).ap()
        self.qk = dt("qk", [2048, S], BF16, kind=kd).ap()
        self.augq = dt("augq", [4, 6, S], BF16, kind=kd).ap()
        self.augk = dt("augk", [4, 6, S], BF16, kind=kd).ap()
        self.vtok = dt("vtok", [S, D], BF16, kind=kd).ap()
        self.OT = dt("OT", [D, S], BF16, kind=kd).ap()
        self.ctd = dt("ctd", [128, S], F32, kind=kd).ap()
        self.std = dt("std", [128, S], F32, kind=kd).ap()
        self.dbg = {}

    def carve(self, shape, dtype):
        n = 1
        for s in shape[1:]:
            n *= s
        nbytes = n * (2 if dtype == BF16 else 4)
        n32 = (nbytes + 3) // 4
        n32 = (n32 + 7) // 8 * 8
        off = self.off
        self.off += n32
        assert self.off <= self.BIG32, ("SBUF overflow", self.off, self.BIG32)
        ap = self.big[:, off:off + n32]
        if dtype == BF16:
            ap = ap.bitcast(BF16)[:, 0:n]
        elif dtype == I32:
            ap = ap.bitcast(I32)[:, 0:n]
        else:
            ap = ap[:, 0:n]
        if len(shape) == 3:
            ap = ap.rearrange("p (a b) -> p a b", a=shape[1])
        elif len(shape) == 4:
            ap = ap.rearrange("p (a b c) -> p a b c", a=shape[1], b=shape[2])
        return ap

    def new_stage(self):
        self.P.barrier()
        self.off = 0

    def load_w(self, dst, src3, key, nsplit):
        P = self.P
        kcs = dst.shape[1]
        pairs = []
        for kc in range(kcs):
            pairs.append((dst[:, kc, :], src3[kc * 128:(kc + 1) * 128, :]))
        P.dma_many("pool", ("w", key), pairs, writes=[key])

    def norm_a(self, ht, hkey, sq):
        n = sq.shape[1]
        self.P.op("act", lambda e: e.activation(out=sq, in_=ht[:, 0:n, :], func=AF.Square), reads=[hkey], writes=["sq"])

    def norm_b(self, ht, hkey, gcol, xn, xkey, sq, rs):
        P = self.P
        ss = self.ps[7]
        cols = self.cols
        n = sq.shape[1]
        for c in range(8):
            if c > 0 and c % n == 0:
                P.op("act", lambda e, c=c: e.activation(out=sq, in_=ht[:, c:c + n, :], func=AF.Square), reads=[hkey], writes=["sq"])
            P.op("pe", lambda e, c=c: e.matmul(ss, lhsT=self.ones_bf, rhs=sq[:, c % n, :], start=(c == 0), stop=(c == 7)),
                 reads=["sq"], writes=["ps7"], inc=(c % n == n - 1))
        P.op("act", lambda e: e.activation(out=rs, in_=ss, func=AF.Ln, bias=self.eps_col, scale=1.0 / D),
             reads=["ps7"], writes=["rs"])
        P.op("act", lambda e: e.activation(out=rs, in_=rs, func=AF.Exp, scale=-0.5), reads=["rs"], writes=["rs"])
        for c in range(8):
            P.op("dve", lambda e, c=c: e.scalar_tensor_tensor(out=xn[:, c, :], in0=ht[:, c, :], scalar=cols[:, gcol + c:gcol + c + 1],
                                                             in1=rs, op0=ALU.mult, op1=ALU.mult),
                 reads=[hkey, "rs"], writes=[xkey])

    def norm(self, ht, hkey, gcol, xn, xkey, sq, rs):
        self.norm_a(ht, hkey, sq)
        self.norm_b(ht, hkey, gcol, xn, xkey, sq, rs)

    def h_src(self):
        return self.h_cur

    def load_h(self, slot, ti, ht):
        P = self.P
        src = self.h_cur.rearrange("(c p) t -> p c t", p=128)[:, :, ti * TT:(ti + 1) * TT]
        P.dma("sp", ("ldh", slot), ht, src, reads=[("hd", ti)], writes=[("ht", slot)])

    def store_h(self, slot, ti, ht, dst):
        P = self.P
        d = dst.rearrange("(c p) t -> p c t", p=128)[:, :, ti * TT:(ti + 1) * TT]
        P.dma("sp", ("sth", slot), d, ht, reads=[("ht", slot)], writes=[("hd", ti)])

    def stage_ffn(self, l, wi_d, wo_d, gcol, dst):
        P = self.P
        self.new_stage()
        wi = self.carve([128, 8, 2 * DFF], BF16)
        wo = self.carve([128, NFC, D], BF16)
        hts = [self.carve([128, 8, TT], F32) for _ in range(2)]
        xn = self.carve([128, 8, TT], BF16)
        hm = self.carve([128, NFC, TT], BF16)
        sq = self.carve([128, 4, TT], BF16)
        sa = self.carve([128, TT], F32)
        rs = self.carve([128, TT], F32)
        wsrc = wi_d[l].rearrange("(kc p) n -> p kc n", p=128)
        for G in range(NFC // 2):
            c0 = 256 * G
            P.dma_many("pool", ("w", "wi", G), [(wi[:, :, c0:c0 + 256], wsrc[:, :, c0:c0 + 256]),
                                                (wi[:, :, DFF + c0:DFF + c0 + 256], wsrc[:, :, DFF + c0:DFF + c0 + 256])], writes=[("wi", G)])
        for G in range((NFC + 3) // 4):
            P.dma_many("pool", ("w", "wo", G), [(wo[:, fc, :], wo_d[l][fc * 128:(fc + 1) * 128, :]) for fc in range(4 * G, min(4 * G + 4, NFC))],
                       writes=[("wo", G)])
        ps = self.ps
        self.load_h(0, 0, hts[0])
        for ti in range(self.NT):
            slot = ti % 2
            ht = hts[slot]
            hkey = ("ht", slot)
            if ti + 1 < self.NT:
                self.load_h(1 - slot, ti + 1, hts[1 - slot])
            if ti == 0:
                self.norm(ht, hkey, gcol, xn, "xn", sq, rs)
            for fc in range(NFC):
                pa = ps[(fc % 2) * 2]
                pg = ps[(fc % 2) * 2 + 1]
                ka, kg = "ps%d" % ((fc % 2) * 2), "ps%d" % ((fc % 2) * 2 + 1)
                for kc in range(8):
                    P.op("pe", lambda e, kc=kc, fc=fc, pa=pa: e.matmul(pa, lhsT=wi[:, kc, fc * 128:(fc + 1) * 128], rhs=xn[:, kc, :],
                                                                         start=(kc == 0), stop=(kc == 7)),
                         reads=[("wi", fc // 2), "xn"], writes=[ka], inc=(kc == 7))
                for kc in range(8):
                    P.op("pe", lambda e, kc=kc, fc=fc, pg=pg: e.matmul(pg, lhsT=wi[:, kc, DFF + fc * 128:DFF + (fc + 1) * 128], rhs=xn[:, kc, :],
                                                                         start=(kc == 0), stop=(kc == 7)),
                         reads=[("wi", fc // 2), "xn"], writes=[kg], inc=(kc == 7))
                P.op("act", lambda e, pa=pa: e.activation(out=sa, in_=pa, func=AF.Silu), reads=[ka], writes=["sa"])
                P.op("dve", lambda e, pg=pg, fc=fc: e.tensor_tensor(out=hm[:, fc, :], in0=pg, in1=sa, op=ALU.mult),
                     reads=[kg, "sa"], writes=["hm"])
            for dc in range(8):
                if dc == 0 and ti + 1 < self.NT:
                    self.norm_a(hts[1 - slot], ("ht", 1 - slot), sq)
                if dc == 3 and ti + 1 < self.NT:
                    self.norm_b(hts[1 - slot], ("ht", 1 - slot), gcol, xn, "xn", sq, rs)
                po = ps[4 + dc % 2]
                ko = "ps%d" % (4 + dc % 2)
                for fc in range(NFC):
                    P.op("pe", lambda e, fc=fc, dc=dc, po=po: e.matmul(po, lhsT=wo[:, fc, dc * 128:(dc + 1) * 128], rhs=hm[:, fc, :],
                                                                         start=(fc == 0), stop=(fc == NFC - 1)),
                         reads=[("wo", fc // 4), "hm"], writes=[ko], inc=(fc == NFC - 1))
                P.op("dve", lambda e, dc=dc, po=po, ht=ht: e.scalar_tensor_tensor(out=ht[:, dc, :], in0=po, scalar=0.5, in1=ht[:, dc, :],
                                                                                 op0=ALU.mult, op1=ALU.add),
                     reads=[ko, hkey], writes=[hkey])
            self.store_h(slot, ti, ht, dst)
        self.h_cur = dst


    def stage_setup(self):
        P = self.P
        self.new_stage()
        S = self.S
        posi = self.carve([128, S], I32)
        ang = self.carve([128, S], F32)
        tmp = self.carve([128, S], F32)
        tmp2 = self.carve([128, S], F32)
        onesr = self.carve([128, S], BF16)
        P.dma("sp", ("ld", "pos"), posi, self.posb, writes=["posi"])
        P.op("dve", lambda e: e.tensor_copy(out=ang, in_=posi), reads=["posi"], writes=["ang"])
        P.op("dve", lambda e: e.tensor_scalar(out=ang, in0=ang, scalar1=self.invf_col, scalar2=None, op0=ALU.mult),
             reads=["ang"], writes=["ang"])
        two_pi = 2.0 * math.pi
        C1 = 6.28125
        C2 = two_pi - C1
        ki = posi

        def sincos(dst, dkey, shift, stkey, dram):
            if shift != 0.0:
                P.op("dve", lambda e: e.tensor_scalar(out=tmp, in0=ang, scalar1=shift, scalar2=None, op0=ALU.add), reads=["ang"], writes=["tmp"])
                srcang = tmp
            else:
                P.op("dve", lambda e: e.tensor_copy(out=tmp, in_=ang), reads=["ang"], writes=["tmp"])
                srcang = tmp
            P.op("dve", lambda e: e.tensor_scalar(out=dst, in0=srcang, scalar1=1.0 / two_pi, scalar2=None, op0=ALU.mult), reads=["tmp"], writes=[dkey])
            P.op("dve", lambda e: e.tensor_copy(out=ki, in_=dst), reads=[dkey], writes=["posi"])
            P.op("dve", lambda e: e.tensor_copy(out=dst, in_=ki), reads=["posi"], writes=[dkey])
            P.op("dve", lambda e: e.scalar_tensor_tensor(out=tmp, in0=dst, scalar=-C1, in1=tmp, op0=ALU.mult, op1=ALU.add), reads=[dkey, "tmp"], writes=["tmp"])
            P.op("dve", lambda e: e.scalar_tensor_tensor(out=tmp, in0=dst, scalar=-C2, in1=tmp, op0=ALU.mult, op1=ALU.add), reads=[dkey, "tmp"], writes=["tmp"])
            P.op("dve", lambda e: e.tensor_scalar(out=dst, in0=tmp, scalar1=math.pi, scalar2=two_pi, op0=ALU.is_gt, op1=ALU.mult), reads=["tmp"], writes=[dkey])
            P.op("dve", lambda e: e.tensor_tensor(out=tmp, in0=tmp, in1=dst, op=ALU.subtract), reads=["tmp", dkey], writes=["tmp"])
            P.op("dve", lambda e: e.tensor_scalar(out=dst, in0=tmp, scalar1=-math.pi, scalar2=two_pi, op0=ALU.is_lt, op1=ALU.mult), reads=["tmp"], writes=[dkey])
            P.op("dve", lambda e: e.tensor_tensor(out=tmp, in0=tmp, in1=dst, op=ALU.add), reads=["tmp", dkey], writes=["tmp"])
            P.op("act", lambda e: e.activation(out=dst, in_=tmp, func=AF.Sin), reads=["tmp"], writes=[dkey])
            P.dma("sp", ("st", stkey), dram, dst, reads=[dkey], writes=[stkey])

        sincos(tmp2, "tmp2", 0.0, "std", self.std)
        sincos(tmp2, "tmp2", 0.5 * math.pi, "ctd", self.ctd)
        P.op("pool", lambda e: e.memset(onesr, 1.0), writes=["onesr"])
        pairs = []
        for i in range(3):
            pairs.append((self.augk[:, i, :], onesr[0:4, :]))
            pairs.append((self.augq[:, 3 + i, :], onesr[0:4, :]))
        P.dma_many("sp", ("st", "ones"), pairs, reads=["onesr"], writes=["augones"])

    def stage_proj(self, l):
        P = self.P
        self.new_stage()
        S = self.S
        cols = self.cols
        w_in = self.w["w_in"][l].rearrange("(kc p) n -> p kc n", p=128)
        wqk = self.carve([128, 8, 2048], BF16)
        wv = self.carve([128, 8, 1024], BF16)
        wf = self.carve([128, 8, 4], BF16)
        nlf = self.carve([128, S], F32)
        off_mark = self.off
        hts = [self.carve([128, 8, TT], F32) for _ in range(2)]
        xns = [self.carve([128, 8, TT], BF16) for _ in range(2)]
        sq = self.carve([128, 8, TT], BF16)
        rs = self.carve([128, TT], F32)
        cts = [self.carve([128, TT], F32) for _ in range(2)]
        sts = [self.carve([128, TT], F32) for _ in range(2)]
        sqb = [self.carve([128, TT], BF16) for _ in range(3)]
        rs2 = [self.carve([128, TT], F32) for _ in range(3)]
        qn32 = [self.carve([128, TT], F32) for _ in range(3)]
        qnb = [self.carve([128, TT], BF16) for _ in range(3)]
        t1 = [self.carve([128, TT], F32) for _ in range(3)]
        t2 = [self.carve([128, TT], F32) for _ in range(3)]
        outq = [self.carve([128, TT], BF16) for _ in range(6)]
        vout = [self.carve([128, 1024], BF16) for _ in range(2)]
        gsc = self.carve([128, 4], F32)
        negb = self.carve([128, 1], F32)
        e4 = self.carve([128, TT], F32)
        pairs = []
        for (d0, s0, wd) in [(0, 0, 256), (256, 256, 256), (512, 768, 256), (768, 1024, 256), (1024, 1540, 512), (1536, 2052, 512)]:
            pairs.append((wqk[:, :, d0:d0 + wd], w_in[:, :, s0:s0 + wd]))
        P.dma_many("pool", ("w", "wqk"), pairs, writes=["wqk"])
        pairs = []
        for (d0, s0, wd) in [(0, 512, 256), (256, 1280, 256), (512, 2564, 512)]:
            pairs.append((wv[:, :, d0:d0 + wd], w_in[:, :, s0:s0 + wd]))
        pairs.append((wf, w_in[:, :, 1536:1540]))
        P.dma_many("pool", ("w", "wv"), pairs, writes=["wv", "wf"])
        P.op("dve", lambda e: e.tensor_scalar(out=gsc[:, 0:1], in0=cols[:, 32:33], scalar1=0.125, scalar2=None, op0=ALU.mult), writes=["gsc"])
        P.op("dve", lambda e: e.tensor_copy(out=gsc[:, 1:2], in_=cols[:, 33:34]), writes=["gsc"])
        P.op("dve", lambda e: e.tensor_scalar(out=gsc[:, 2:3], in0=cols[:, 34:35], scalar1=0.125, scalar2=None, op0=ALU.mult), writes=["gsc"])
        P.op("dve", lambda e: e.tensor_copy(out=gsc[:, 3:4], in_=cols[:, 35:36]), writes=["gsc"])
        P.op("dve", lambda e: e.tensor_scalar(out=negb, in0=cols[:, 37:38], scalar1=-1.0, scalar2=None, op0=ALU.mult), writes=["negb"])
        ps = self.ps
        self.load_h(0, 0, hts[0])
        oslot = 0
        for ti in range(self.NT):
            slot = ti % 2
            ht = hts[slot]
            hkey = ("ht", slot)
            tsl = slice(ti * TT, (ti + 1) * TT)
            ct, stt = cts[slot], sts[slot]
            P.dma("sp", ("ldc", slot), ct, self.ctd[:, tsl], reads=["ctd"], writes=[("ct", slot)])
            P.dma("sp", ("lds", slot), stt, self.std[:, tsl], reads=["std"], writes=[("st", slot)])
            if ti + 1 < self.NT:
                self.load_h(1 - slot, ti + 1, hts[1 - slot])
            xn, xkey = xns[slot], ("xn", slot)
            if ti == 0:
                self.norm(ht, hkey, 8, xn, xkey, sq, rs)
            def MM(idx):
                pq, kq = ps[idx % 4], "ps%d" % (idx % 4)
                for kc in range(8):
                    P.op("pe", lambda e, kc=kc, xn=xn: e.matmul(pq, lhsT=wqk[:, kc, idx * 128:(idx + 1) * 128], rhs=xn[:, kc, :],
                                                        start=(kc == 0), stop=(kc == 7)),
                         reads=["wqk", xkey], writes=[kq], inc=(kc == 7))

            def POST1(idx, oq, okey, osl):
                pq, kq = ps[idx % 4], "ps%d" % (idx % 4)
                b3 = idx % 3
                if idx < 4:
                    sc = 0.125 if idx < 2 else 1.0
                    P.op("act", lambda e: e.activation(out=oq, in_=pq, func=AF.Copy, scale=sc), reads=[kq], writes=[okey])
                    P.dma("sp", ("stq", osl), self.qk[idx * 128:(idx + 1) * 128, tsl], oq, reads=[okey], writes=[("qkd", idx, ti)])
                    return
                gi = (0 if idx < 6 else 1) if idx < 8 else (2 if idx < 12 else 3)
                pss, kss = ps[4], "ps4"
                P.op("act", lambda e: e.activation(out=sqb[b3], in_=pq, func=AF.Square), reads=[kq], writes=[("sqb", b3)])
                P.op("pe", lambda e: e.matmul(pss, lhsT=self.bd_bf, rhs=sqb[b3], start=True, stop=True), reads=[("sqb", b3)], writes=[kss])
                P.op("act", lambda e: e.activation(out=rs2[b3], in_=pss, func=AF.Ln, bias=self.eps_col, scale=1.0 / 64), reads=[kss], writes=[("rs2", b3)])
                P.op("act", lambda e: e.activation(out=rs2[b3], in_=rs2[b3], func=AF.Exp, scale=-0.5), reads=[("rs2", b3)], writes=[("rs2", b3)])
                if idx < 8:
                    P.op("dve", lambda e: e.scalar_tensor_tensor(out=oq, in0=pq, scalar=gsc[:, gi:gi + 1], in1=rs2[b3], op0=ALU.mult, op1=ALU.mult),
                         reads=[kq, ("rs2", b3), "gsc"], writes=[okey])
                    P.dma("sp", ("stq", osl), self.qk[idx * 128:(idx + 1) * 128, tsl], oq, reads=[okey], writes=[("qkd", idx, ti)])
                else:
                    P.op("dve", lambda e: e.scalar_tensor_tensor(out=qn32[b3], in0=pq, scalar=gsc[:, gi:gi + 1], in1=rs2[b3], op0=ALU.mult, op1=ALU.mult),
                         reads=[kq, ("rs2", b3), "gsc"], writes=[("qn32", b3)])
                    P.op("act", lambda e: e.activation(out=qnb[b3], in_=qn32[b3], func=AF.Copy), reads=[("qn32", b3)], writes=[("qnb", b3)])

            def POST2(idx, oq, okey, osl):
                if idx < 8:
                    return
                b3 = idx % 3
                ppm, kpm = ps[5], "ps5"
                P.op("pe", lambda e: e.matmul(ppm, lhsT=self.pm_bf, rhs=qnb[b3], start=True, stop=True), reads=[("qnb", b3)], writes=[kpm])
                P.op("pool", lambda e, ct=ct: e.tensor_tensor(out=t1[b3], in0=qn32[b3], in1=ct, op=ALU.mult), reads=[("qn32", b3), ("ct", slot)], writes=[("t1", b3)])
                P.op("dve", lambda e, stt=stt: e.tensor_tensor(out=t2[b3], in0=ppm, in1=stt, op=ALU.mult), reads=[kpm, ("st", slot)], writes=[("t2", b3)])
                P.op("pool", lambda e: e.tensor_tensor(out=oq, in0=t1[b3], in1=t2[b3], op=ALU.add), reads=[("t1", b3), ("t2", b3)], writes=[okey])
                P.dma("sp", ("stq", osl), self.qk[idx * 128:(idx + 1) * 128, tsl], oq, reads=[okey], writes=[("qkd", idx, ti)])

            oinfo = {}
            for step in range(16 + 2):
                if step == 6 and ti + 1 < self.NT:
                    self.norm_a(hts[1 - slot], ("ht", 1 - slot), sq)
                if step == 9 and ti + 1 < self.NT:
                    self.norm_b(hts[1 - slot], ("ht", 1 - slot), 8, xns[1 - slot], ("xn", 1 - slot), sq, rs)
                if step < 16:
                    MM(step)
                if 1 <= step <= 16:
                    i1 = step - 1
                    osl = oslot % 6
                    oslot += 1
                    oinfo[i1] = (outq[osl], ("oq", osl), osl)
                    POST1(i1, *oinfo[i1])
                if step >= 2:
                    i2 = step - 2
                    POST2(i2, *oinfo[i2])
            for sub in range(4):
                vs = (ti * 4 + sub) % 2
                vo = vout[vs]
                for half in range(2):
                    vb_ = 6 + (sub * 2 + half) % 2
                    pv, kv = ps[vb_], "ps%d" % vb_
                    for kc in range(8):
                        P.op("pe", lambda e, kc=kc, sub=sub, half=half, pv=pv, xn=xn: e.matmul(pv, lhsT=xn[:, kc, sub * 128:(sub + 1) * 128],
                                                                                       rhs=wv[:, kc, half * 512:(half + 1) * 512],
                                                                                       start=(kc == 0), stop=(kc == 7)),
                             reads=["wv", xkey], writes=[kv], inc=(kc == 7))
                    if half == 0:
                        P.op("act", lambda e, pv=pv, vo=vo: e.activation(out=vo[:, 0:512], in_=pv, func=AF.Copy), reads=[kv], writes=[("vo", vs)])
                    else:
                        P.op("dve", lambda e, pv=pv, vo=vo: e.tensor_copy(out=vo[:, 512:1024], in_=pv), reads=[kv], writes=[("vo", vs)])
                r0 = ti * TT + sub * 128
                P.dma("sp", ("stv", vs), self.vtok[r0:r0 + 128, :], vo, reads=[("vo", vs)], writes=[("vd", ti, sub)])
            pf = ps[7]
            for kc in range(8):
                P.op("pe", lambda e, kc=kc, xn=xn: e.matmul(pf[0:4, :], lhsT=wf[:, kc, :], rhs=xn[:, kc, :], start=(kc == 0), stop=(kc == 7)),
                     reads=["wf", xkey], writes=["ps7"], inc=(kc == 7))
            P.op("act", lambda e: e.activation(out=e4[0:4, :], in_=pf[0:4, :], func=AF.Exp, bias=negb[0:4, :], scale=-1.0),
                 reads=["ps7", "negb"], writes=["e4"])
            P.op("act", lambda e, tsl=tsl: e.activation(out=nlf[0:4, tsl], in_=e4[0:4, :], func=AF.Ln, bias=1.0, scale=1.0),
                 reads=["e4"], writes=["nlf"])
        P.barrier()
        self.off = off_mark
        cumn = self.carve([128, S], F32)
        pcs = [self.carve([128, S], BF16) for _ in range(3)]
        npcs = [self.carve([128, S], BF16) for _ in range(3)]
        p32 = self.carve([128, S], F32)
        A4 = slice(0, 4)
        P.op("dve", lambda e: e.tensor_tensor_scan(out=cumn[A4, :], data0=nlf[A4, :], data1=nlf[A4, :], initial=0.0, op0=ALU.add, op1=ALU.bypass),
             reads=["nlf"], writes=["cumn"])
        for i in range(3):
            P.op("dve", lambda e, i=i: e.tensor_copy(out=pcs[i][A4, :], in_=cumn[A4, :]), reads=["cumn"], writes=[("pc", i)])
            P.op("dve", lambda e, i=i: e.tensor_scalar(out=npcs[i][A4, :], in0=pcs[i][A4, :], scalar1=-1.0, scalar2=None, op0=ALU.mult),
                 reads=[("pc", i)], writes=[("npc", i)])
            if i < 2:
                P.op("dve", lambda e, i=i: e.tensor_copy(out=p32[A4, :], in_=pcs[i][A4, :]), reads=[("pc", i)], writes=["p32"])
                P.op("dve", lambda e: e.tensor_tensor(out=cumn[A4, :], in0=cumn[A4, :], in1=p32[A4, :], op=ALU.subtract),
                     reads=["cumn", "p32"], writes=["cumn"])
        pairs = []
        for i in range(3):
            pairs.append((self.augk[:, 3 + i, :], pcs[i][A4, :]))
            pairs.append((self.augq[:, i, :], npcs[i][A4, :]))
        P.dma_many("sp", ("st", "aug"), pairs, reads=[("pc", 0), ("pc", 1), ("pc", 2), ("npc", 0), ("npc", 1), ("npc", 2)], writes=["augd"])


    def stage_attn(self, l):
        P = self.P
        self.new_stage()
        S, NB, NT = self.S, self.NB, self.NT
        cols = self.cols
        lam_init = self.lam_inits[l]
        ps = self.ps
        masks = self.carve([128, 12, 512], BF16)
        dl = self.carve([128, 256], F32)
        prod = self.carve([128, 64], F32)
        s12 = self.carve([128, 2], F32)
        e12 = self.carve([128, 2], F32)
        neglam = self.carve([128, 1], F32)
        gsub = self.carve([128, 1], F32)
        qb = [self.carve([128, S], BF16) for _ in range(2)]
        kb = [self.carve([128, S], BF16) for _ in range(2)]
        vaug = [self.carve([128, NB, 128], BF16) for _ in range(2)]
        vdf = [self.carve([128, NB, 128], BF16) for _ in range(2)]
        e32 = [self.carve([128, 1024], F32) for _ in range(2)]
        spb = [self.carve([128, 1024], BF16) for _ in range(3)]
        u = [self.carve([128, 1024], F32) for _ in range(2)]
        w = [self.carve([128, 1024], F32) for _ in range(2)]
        A2 = [self.carve([128, 1024], BF16) for _ in range(2)]
        A4 = [[self.carve([128, 1024], BF16) for _ in range(2)] for _ in range(2)]
        accb = [self.carve([128, 512], BF16) for _ in range(3)]
        accm = [self.carve([128, 512], BF16) for _ in range(2)]
        rinv = self.carve([128, 512], F32)
        a1 = self.carve([128, 512], F32)
        a2 = self.carve([128, 512], F32)
        oc = self.carve([128, 512], F32)
        sqo = self.carve([128, 512], BF16)
        rso = self.carve([128, 512], F32)
        obuf = [self.carve([128, 512], BF16) for _ in range(2)]
        cpy = [[self.carve([128, 512], F32) for _ in range(4)] for _ in range(2)]
        P.dma_many("pool", ("w", "masks"), [(masks[:, i, :], self.cbf_d[:, 640 + i * 512:640 + (i + 1) * 512]) for i in range(12)], writes=["masks"])
        P.dma("sp", ("ld", "dl"), dl, self.dlam_d[l], writes=["dl"])
        for sl in range(2):
            P.op("pool", lambda e, sl=sl: e.memset(vaug[sl][:, :, 64:128], 1.0), writes=[("vones", sl)])
        P.op("dve", lambda e: e.tensor_tensor(out=prod, in0=dl[:, 0:64], in1=dl[:, 64:128], op=ALU.mult), reads=["dl"], writes=["prod"])
        P.op("dve", lambda e: e.tensor_reduce(out=s12[:, 0:1], in_=prod, axis=mybir.AxisListType.X, op=ALU.add), reads=["prod"], writes=["s12"])
        P.op("dve", lambda e: e.tensor_tensor(out=prod, in0=dl[:, 128:192], in1=dl[:, 192:256], op=ALU.mult), reads=["dl", "s12"], writes=["prod"])
        P.op("dve", lambda e: e.tensor_reduce(out=s12[:, 1:2], in_=prod, axis=mybir.AxisListType.X, op=ALU.add), reads=["prod"], writes=["s12"])
        P.op("act", lambda e: e.activation(out=e12, in_=s12, func=AF.Exp), reads=["s12"], writes=["e12"])
        P.op("dve", lambda e: e.tensor_tensor(out=neglam, in0=e12[:, 1:2], in1=e12[:, 0:1], op=ALU.subtract), reads=["e12"], writes=["neglam"])
        P.op("dve", lambda e: e.tensor_scalar(out=neglam, in0=neglam, scalar1=-lam_init, scalar2=None, op0=ALU.add), reads=["neglam"], writes=["neglam"])
        P.op("dve", lambda e: e.tensor_scalar(out=gsub, in0=cols[:, 36:37], scalar1=1.0 - lam_init, scalar2=None, op0=ALU.mult), writes=["gsub"])

        heads = [("sb", h) for h in range(4)] + [("fox", h) for h in range(4)] + [("diff", h) for h in range(4)]
        vt = self.vtok.rearrange("(b p) c -> p b c", p=128)
        qk = self.qk

        def load_head(n):
            typ, h = heads[n]
            sl = n % 2
            pairs = []
            hh = (h % 2) * 64
            nvs = max(1, NB // 8)

            def vpairs(dst, c0, wd):
                for j in range(nvs):
                    b0, b1 = j * NB // nvs, (j + 1) * NB // nvs
                    pairs.append((dst[:, b0:b1, 0:wd], vt[:, b0:b1, c0:c0 + wd]))
            if typ == "sb":
                r = (h // 2) * 128 + hh
                pairs.append((qb[sl][0:64, :], qk[r:r + 64, :]))
                r = (2 + h // 2) * 128 + hh
                pairs.append((kb[sl][0:64, :], qk[r:r + 64, :]))
                vpairs(vaug[sl], h * 64, 64)
            elif typ == "fox":
                r = (4 + h // 2) * 128 + hh
                pairs.append((qb[sl][0:64, :], qk[r:r + 64, :]))
                r = (6 + h // 2) * 128 + hh
                pairs.append((kb[sl][0:64, :], qk[r:r + 64, :]))
                pairs.append((qb[sl][64:70, :], self.augq[h]))
                pairs.append((kb[sl][64:70, :], self.augk[h]))
                vpairs(vaug[sl], 256 + h * 64, 64)
            else:
                r = (8 + h) * 128
                pairs.append((qb[sl], qk[r:r + 128, :]))
                r = (12 + h) * 128
                pairs.append((kb[sl], qk[r:r + 128, :]))
                vpairs(vdf[sl], 512 + h * 128, 128)
            wr = [("hq", sl), ("hv" if typ != "diff" else "hvd", sl)]
            P.dma_many("sp", ("ldhd", sl), pairs, reads=[("vones", sl)], writes=wr)

        ostate = {"n": 0}
        psp = self.psp
        ZP = [(0, 1), (6, 7)]
        zpair = [psp[0], psp[3]]

        def score(sl, pr, typ, qi, kbi, zb):
            t0 = qi * TT
            zp, zk = ps[zb], "ps%d" % zb
            diag = kbi >= 4 * qi
            P.op("pe", lambda e: e.matmul(zp, lhsT=kb[sl][pr, kbi * 128:(kbi + 1) * 128], rhs=qb[sl][pr, t0:t0 + TT], start=True, stop=(not diag)),
                 reads=[("hq", sl)], writes=[zk], inc=(not diag))
            if diag:
                m = kbi - 4 * qi
                P.op("pe", lambda e: e.matmul(zp, lhsT=self.id_bf, rhs=masks[:, typ * 4 + m, :], start=False, stop=True),
                     reads=["masks"], writes=[zk])

        def softmax_pass(sl, pr, typ, qi, vfn, targets):
            nkb = 4 * qi + 4
            npair = nkb // 2
            for jp in range(npair + 1):
                if jp < npair:
                    pp = jp % 2
                    for half in range(2):
                        score(sl, pr, typ, qi, 2 * jp + half, ZP[pp][half])
                if jp >= 1:
                    j = jp - 1
                    pp = j % 2
                    zkeys = ["ps%d" % ZP[pp][0], "ps%d" % ZP[pp][1]]
                    P.op("act", lambda e, pp=pp: e.activation(out=A2[pp], in_=zpair[pp], func=AF.Exp), reads=zkeys, writes=[("A", pp)])
                    for half in range(2):
                        blk = 2 * j + half
                        for ti_, (tp, tk, lf) in enumerate(targets):
                            lastt = (ti_ == len(targets) - 1)
                            P.op("pe", lambda e, tp=tp, lf=lf, blk=blk, pp=pp, half=half: e.matmul(tp, lhsT=lf(blk), rhs=A2[pp][:, half * TT:(half + 1) * TT],
                                                                                                 start=(blk == 0), stop=(blk == nkb - 1)),
                                 reads=[("A", pp), vfn], writes=[tk], inc=(blk == nkb - 1 and lastt))

        def sb_head(sl, h):
            pr = slice(0, 64)
            TB = (4, 5)
            tailp = psp[2]
            items = [(qi, j) for qi in range(NT) for j in range((4 * qi + 4) // 2)]
            F = len(items)

            def info(f):
                qi, j = items[f]
                nkb = 4 * qi + 4
                return qi, j, nkb, nkb // 2

            def stA(f):
                qi, j, nkb, npair = info(f)
                pp = f % 2
                for half in range(2):
                    score(sl, pr, 0, qi, nkb - 1 - (2 * j + half), ZP[pp][half])

            def stB(f):
                pp = f % 2
                zkeys = ["ps%d" % ZP[pp][0], "ps%d" % ZP[pp][1]]
                P.op("act", lambda e: e.activation(out=e32[pp], in_=zpair[pp], func=AF.Exp), reads=zkeys, writes=[("e32", pp)])
                P.op("act", lambda e: e.activation(out=spb[f % 3], in_=e32[pp], func=AF.Ln, bias=1.0, scale=1.0),
                     reads=[("e32", pp)], writes=[("sp", f % 3)])

            def stC(f):
                qi, j, nkb, npair = info(f)
                pp = f % 2
                zkeys = ["ps%d" % ZP[pp][0], "ps%d" % ZP[pp][1]]
                P.op("dve", lambda e: e.tensor_tensor(out=u[pp], in0=zpair[pp], in1=spb[f % 3], op=ALU.subtract),
                     reads=zkeys + [("sp", f % 3)], writes=[("u", pp)])
                s3 = spb[f % 3]
                if j == 0:
                    if j + 1 < npair:
                        P.op("pool", lambda e: e.tensor_tensor(out=accb[(f + 1) % 3], in0=s3[:, 0:TT], in1=s3[:, TT:2 * TT], op=ALU.add),
                             reads=[("sp", f % 3)], writes=[("acc", (f + 1) % 3)])
                else:
                    P.op("pool", lambda e: e.tensor_tensor(out=accm[f % 2], in0=accb[f % 3], in1=s3[:, 0:TT], op=ALU.add),
                         reads=[("acc", f % 3), ("sp", f % 3)], writes=[("accm", f % 2)])
                    if j + 1 < npair:
                        P.op("pool", lambda e: e.tensor_tensor(out=accb[(f + 1) % 3], in0=accm[f % 2], in1=s3[:, TT:2 * TT], op=ALU.add),
                             reads=[("accm", f % 2), ("sp", f % 3)], writes=[("acc", (f + 1) % 3)])

            def stD(f):
                qi, j, nkb, npair = info(f)
                s3 = spb[f % 3]
                rd = [("sp", f % 3)] + ([("acc", f % 3), ("accm", f % 2)] if j > 0 else [])
                P.op("pe", lambda e: e.matmul(ps[TB[0]], lhsT=self.tri_bf, rhs=s3[:, 0:TT], start=True, stop=(j == 0)),
                     reads=rd, writes=["ps%d" % TB[0]], inc=False)
                if j > 0:
                    P.op("pe", lambda e: e.matmul(ps[TB[0]], lhsT=self.ones_bf, rhs=accb[f % 3], start=False, stop=True),
                         reads=rd, writes=["ps%d" % TB[0]], inc=False)
                P.op("pe", lambda e: e.matmul(ps[TB[1]], lhsT=self.tri_bf, rhs=s3[:, TT:2 * TT], start=True, stop=False),
                     reads=rd, writes=["ps%d" % TB[1]], inc=False)
                second = s3[:, 0:TT] if j == 0 else accm[f % 2]
                P.op("pe", lambda e: e.matmul(ps[TB[1]], lhsT=self.ones_bf, rhs=second, start=False, stop=True),
                     reads=rd, writes=["ps%d" % TB[1]], inc=True)

            def stE(f):
                pp = f % 2
                P.op("dve", lambda e: e.tensor_tensor(out=w[pp], in0=u[pp], in1=tailp, op=ALU.subtract),
                     reads=[("u", pp), "ps%d" % TB[0], "ps%d" % TB[1]], writes=[("w", pp)])

            def stF(f):
                pp = f % 2
                P.op("act", lambda e: e.activation(out=A2[pp], in_=w[pp], func=AF.Exp), reads=[("w", pp)], writes=[("A", pp)])

            def stG(f):
                qi, j, nkb, npair = info(f)
                pp = f % 2
                for half in range(2):
                    i = 2 * j + half
                    kbi = nkb - 1 - i
                    P.op("pe", lambda e, i=i, kbi=kbi, half=half: e.matmul(ps[2][0:64, :], lhsT=vaug[sl][:, kbi, 0:64], rhs=A2[pp][:, half * TT:(half + 1) * TT],
                                                                          start=(i == 0), stop=(i == nkb - 1)),
                         reads=[("A", pp), ("hv", sl)], writes=["ps2"], inc=(i == nkb - 1))
                if j == npair - 1:
                    t0 = qi * TT
                    os_ = ostate["n"] % 2
                    ostate["n"] += 1
                    P.op("act", lambda e: e.activation(out=obuf[os_][0:64, :], in_=ps[2][0:64, :], func=AF.Copy), reads=["ps2"], writes=[("ob", os_)])
                    P.dma("sp", ("sto", os_), self.OT[h * 64:(h + 1) * 64, t0:t0 + TT], obuf[os_][0:64, :], reads=[("ob", os_)], writes=[("otd", h, qi)])

            for step in range(F + 2):
                if step < F:
                    stA(step)
                if 1 <= step <= F:
                    stD(step - 1); stE(step - 1)
                if step < F:
                    stB(step)
                if 1 <= step <= F:
                    stF(step - 1)
                if step < F:
                    stC(step)
                if step >= 2:
                    stG(step - 2)

        def fox_head(sl, h):
            pr = slice(0, 70)
            items = [(qi, j) for qi in range(NT) for j in range((4 * qi + 4) // 2)]
            F = len(items)
            for step in range(F + 1):
                if step < F:
                    qi, jp = items[step]
                    pp = step % 2
                    for half in range(2):
                        score(sl, pr, 1, qi, 2 * jp + half, ZP[pp][half])
                if step >= 1:
                    f = step - 1
                    qi, j = items[f]
                    nkb = 4 * qi + 4
                    pp = f % 2
                    zkeys = ["ps%d" % ZP[pp][0], "ps%d" % ZP[pp][1]]
                    P.op("act", lambda e, pp=pp: e.activation(out=A2[pp], in_=zpair[pp], func=AF.Exp), reads=zkeys, writes=[("A", pp)])
                    for half in range(2):
                        blk = 2 * j + half
                        P.op("pe", lambda e, blk=blk, pp=pp, half=half, nkb=nkb: e.matmul(ps[2], lhsT=vaug[sl][:, blk, :], rhs=A2[pp][:, half * TT:(half + 1) * TT],
                                                                                        start=(blk == 0), stop=(blk == nkb - 1)),
                             reads=[("A", pp), ("hv", sl)], writes=["ps2"], inc=(blk == nkb - 1))
                    if j == nkb // 2 - 1:
                        fox_epi(h, qi)

        def fox_epi(h, qi):
            t0 = qi * TT
            os_ = ostate["n"] % 2
            ostate["n"] += 1
            c0 = cpy[os_][0]
            P.op("act", lambda e: e.activation(out=c0, in_=ps[2], func=AF.Copy), reads=["ps2"], writes=[("cp", os_, 0)])
            P.op("dve", lambda e: e.tensor_copy(out=rinv[0:64, :], in_=c0[64:128, :]), reads=[("cp", os_, 0)], writes=["rinv"])
            P.op("dve", lambda e: e.reciprocal(out=rinv[0:64, :], in_=rinv[0:64, :]), reads=["rinv"], writes=["rinv"])
            P.op("dve", lambda e: e.tensor_tensor(out=obuf[os_][0:64, :], in0=c0[0:64, :], in1=rinv[0:64, :], op=ALU.mult),
                 reads=[("cp", os_, 0), "rinv"], writes=[("ob", os_)])
            r = 256 + h * 64
            P.dma("sp", ("sto", os_), self.OT[r:r + 64, t0:t0 + TT], obuf[os_][0:64, :], reads=[("ob", os_)], writes=[("otd", 4 + h, qi)])

        deferred = []

        def tick():
            for d_ in deferred:
                d_[0] -= 1
            while deferred and deferred[0][0] <= 0:
                deferred.pop(0)[1]()

        def flush():
            while deferred:
                deferred.pop(0)[1]()

        def diff_head(sl, h):
            items = [(qi, j) for qi in range(NT) for j in range((4 * qi + 4) // 2)]
            F = len(items)
            for step in range(F + 1):
                tick()
                if step < F:
                    qi, jp = items[step]
                    for half in range(2):
                        for sub in range(2):
                            score(sl, slice(64 * sub, 64 * sub + 64), 2, qi, 2 * jp + half, ZP[sub][half])
                    for sub in range(2):
                        zkeys = ["ps%d" % ZP[sub][0], "ps%d" % ZP[sub][1]]
                        P.op("act", lambda e, sub=sub, step=step: e.activation(out=A4[sub][step % 2], in_=zpair[sub], func=AF.Exp),
                             reads=zkeys, writes=[("A4", sub, step % 2)])
                if step >= 1:
                    f = step - 1
                    qi, j = items[f]
                    nkb = 4 * qi + 4
                    for sub in range(2):
                        ub, rb = 2 + 2 * sub, 3 + 2 * sub
                        for half in range(2):
                            blk = 2 * j + half
                            rhs = A4[sub][f % 2][:, half * TT:(half + 1) * TT]
                            P.op("pe", lambda e, ub=ub, blk=blk, rhs=rhs, nkb=nkb: e.matmul(ps[ub], lhsT=vdf[sl][:, blk, :], rhs=rhs, start=(blk == 0), stop=(blk == nkb - 1)),
                                 reads=[("A4", sub, f % 2), ("hvd", sl)], writes=["ps%d" % ub], inc=False)
                            P.op("pe", lambda e, rb=rb, blk=blk, rhs=rhs, nkb=nkb: e.matmul(ps[rb], lhsT=self.ones_bf, rhs=rhs, start=(blk == 0), stop=(blk == nkb - 1)),
                                 reads=[("A4", sub, f % 2)], writes=["ps%d" % rb], inc=(half == 1))
                    if j == nkb // 2 - 1:
                        diff_epi(h, qi)

        def diff_epi(h, qi):
            t0 = qi * TT
            os_ = ostate["n"] % 2
            ostate["n"] += 1
            cU1, cR1, cU2, cR2 = cpy[os_]
            P.op("act", lambda e: e.activation(out=cU1, in_=ps[2], func=AF.Copy), reads=["ps2"], writes=[("cp", os_, 0)])
            P.op("dve", lambda e: e.tensor_copy(out=cR1, in_=ps[3]), reads=["ps3"], writes=[("cp", os_, 1)])
            P.op("act", lambda e: e.activation(out=cU2, in_=ps[4], func=AF.Copy), reads=["ps4"], writes=[("cp", os_, 2)])
            P.op("dve", lambda e: e.tensor_copy(out=cR2, in_=ps[5]), reads=["ps5"], writes=[("cp", os_, 3)])

            def part2():
                P.op("dve", lambda e: e.reciprocal(out=rinv, in_=cR1), reads=[("cp", os_, 1)], writes=["rinv"])
                P.op("dve", lambda e: e.tensor_tensor(out=a1, in0=cU1, in1=rinv, op=ALU.mult), reads=[("cp", os_, 0), "rinv"], writes=["a1"])
                P.op("dve", lambda e: e.reciprocal(out=rinv, in_=cR2), reads=[("cp", os_, 3), "a1"], writes=["rinv"])
                P.op("dve", lambda e: e.tensor_tensor(out=a2, in0=cU2, in1=rinv, op=ALU.mult), reads=[("cp", os_, 2), "rinv"], writes=["a2"])
                P.op("dve", lambda e: e.scalar_tensor_tensor(out=oc, in0=a2, scalar=neglam, in1=a1, op0=ALU.mult, op1=ALU.add),
                     reads=["a1", "a2", "neglam"], writes=["oc"])
                P.op("act", lambda e: e.activation(out=sqo, in_=oc, func=AF.Square), reads=["oc"], writes=["sqo"])
                P.op("pe", lambda e: e.matmul(ps[6], lhsT=self.ones_bf, rhs=sqo, start=True, stop=True), reads=["sqo"], writes=["ps6"])
                P.op("act", lambda e: e.activation(out=rso, in_=ps[6], func=AF.Ln, bias=self.eps_col, scale=1.0 / 128), reads=["ps6"], writes=["rso"])
                P.op("act", lambda e: e.activation(out=rso, in_=rso, func=AF.Exp, scale=-0.5), reads=["rso"], writes=["rso"])
                P.op("dve", lambda e: e.scalar_tensor_tensor(out=obuf[os_], in0=oc, scalar=gsub, in1=rso, op0=ALU.mult, op1=ALU.mult),
                     reads=["oc", "rso", "gsub"], writes=[("ob", os_)])
                r = 512 + h * 128
                P.dma("sp", ("sto", os_), self.OT[r:r + 128, t0:t0 + TT], obuf[os_], reads=[("ob", os_)], writes=[("otd", 8 + h, qi)])
            deferred.append([4, part2])

        hl = self.head_list if getattr(self, "head_list", None) is not None else list(range(12))
        heads = [heads[i] for i in hl]
        load_head(0)
        for n, (typ, h) in enumerate(heads):
            if n + 1 < len(heads):
                load_head(n + 1)
            sl = n % 2
            if typ == "sb":
                sb_head(sl, h)
            elif typ == "fox":
                fox_head(sl, h)
            else:
                diff_head(sl, h)
            flush()


    def stage_merge(self, l, dst):
        P = self.P
        self.new_stage()
        ps = self.ps
        w_in = self.w["w_in"][l].rearrange("(kc p) n -> p kc n", p=128)
        wg = self.carve([128, 8, 3072], BF16)
        wbr = self.carve([128, 8, D], BF16)
        wo = self.carve([128, 8, D], BF16)
        NS = 3
        hts = [self.carve([128, 8, TT], F32) for _ in range(NS)]
        xns = [self.carve([128, 8, TT], BF16) for _ in range(2)]
        sq = self.carve([128, 8, TT], BF16)
        rs = self.carve([128, TT], F32)
        ots = [self.carve([128, 8, TT], BF16) for _ in range(2)]
        sg = [self.carve([128, TT], F32) for _ in range(3)]
        m1 = self.carve([128, TT], F32)
        m2 = self.carve([128, TT], F32)
        m3 = self.carve([128, TT], F32)
        mg = self.carve([128, 8, TT], BF16)
        P.dma_many("pool", ("w", "wg"), [(wg[:, kc, :], w_in[:, kc, 3076:6148]) for kc in range(8)], writes=["wg"])
        self.load_w(wbr, self.w["w_br"][l], "wbr", 8)
        self.load_w(wo, self.w["w_o"][l], "wo", 8)
        OTv = self.OT.rearrange("(c p) t -> p c t", p=128)
        rcs = [(0, 2), (2, 4), (4, 8)]

        def load_ot(slot, ti):
            P.dma("sp", ("ldot", slot), ots[slot], OTv[:, :, ti * TT:(ti + 1) * TT], writes=[("ot", slot)])
        for t_ in range(min(NS - 1, self.NT)):
            self.load_h(t_ % NS, t_, hts[t_ % NS])
        load_ot(0, 0)
        for ti in range(self.NT):
            slot = ti % NS
            x2 = ti % 2
            ht, ot = hts[slot], ots[x2]
            hkey = ("ht", slot)
            nslot = (ti + 1) % NS
            if ti + NS - 1 < self.NT:
                self.load_h((ti + NS - 1) % NS, ti + NS - 1, hts[(ti + NS - 1) % NS])
            if ti + 1 < self.NT:
                load_ot(1 - x2, ti + 1)
            xn, xkey = xns[x2], ("xn", x2)
            if ti == 0:
                self.norm(ht, hkey, 8, xn, xkey, sq, rs)
            for dc in range(8):
                if dc == 2 and ti + 1 < self.NT:
                    self.norm_a(hts[nslot], ("ht", nslot), sq)
                if dc == 4 and ti + 1 < self.NT:
                    self.norm_b(hts[nslot], ("ht", nslot), 8, xns[1 - x2], ("xn", 1 - x2), sq, rs)
                dsl = slice(dc * 128, (dc + 1) * 128)
                for bi in range(3):
                    for kc in range(8):
                        P.op("pe", lambda e, bi=bi, kc=kc, dc=dc, xn=xn: e.matmul(ps[3 + bi], lhsT=wg[:, kc, bi * 1024 + dc * 128:bi * 1024 + (dc + 1) * 128],
                                                                          rhs=xn[:, kc, :], start=(kc == 0), stop=(kc == 7)),
                             reads=["wg", xkey], writes=["ps%d" % (3 + bi)], inc=(kc == 7))
                for bi, (r0, r1) in enumerate(rcs):
                    for rc in range(r0, r1):
                        P.op("pe", lambda e, bi=bi, rc=rc, dsl=dsl, r0=r0, r1=r1, ot=ot: e.matmul(ps[bi], lhsT=wbr[:, rc, dsl], rhs=ot[:, rc, :],
                                                                                          start=(rc == r0), stop=(rc == r1 - 1)),
                             reads=["wbr", ("ot", x2)], writes=["ps%d" % bi], inc=(rc == r1 - 1))
                for bi in range(3):
                    P.op("act", lambda e, bi=bi: e.activation(out=sg[bi], in_=ps[3 + bi], func=AF.Sigmoid), reads=["ps%d" % (3 + bi)], writes=[("sg", bi)])
                P.op("dve", lambda e: e.tensor_tensor(out=m1, in0=ps[0], in1=sg[0], op=ALU.mult), reads=["ps0", ("sg", 0)], writes=["m1"])
                P.op("dve", lambda e: e.tensor_tensor(out=m2, in0=ps[1], in1=sg[1], op=ALU.mult), reads=["ps1", ("sg", 1)], writes=["m2"])
                P.op("dve", lambda e: e.tensor_tensor(out=m3, in0=ps[2], in1=sg[2], op=ALU.mult), reads=["ps2", ("sg", 2)], writes=["m3"])
                P.op("pool", lambda e: e.tensor_tensor(out=m1, in0=m1, in1=m2, op=ALU.add), reads=["m1", "m2"], writes=["m1"])
                P.op("pool", lambda e, dc=dc: e.tensor_tensor(out=mg[:, dc, :], in0=m1, in1=m3, op=ALU.add), reads=["m1", "m3"], writes=["mg"])
            for dc in range(8):
                dsl = slice(dc * 128, (dc + 1) * 128)
                pb = 6 + dc % 2
                pk = "ps%d" % pb
                for mc in range(8):
                    P.op("pe", lambda e, mc=mc, dsl=dsl, pb=pb: e.matmul(ps[pb], lhsT=wo[:, mc, dsl], rhs=mg[:, mc, :], start=(mc == 0), stop=(mc == 7)),
                         reads=["wo", "mg"], writes=[pk], inc=(mc == 7))
                P.op("dve", lambda e, dc=dc, ht=ht, pb=pb: e.tensor_tensor(out=ht[:, dc, :], in0=ps[pb], in1=ht[:, dc, :], op=ALU.add),
                     reads=[pk, hkey], writes=[hkey])
            self.store_h(slot, ti, ht, dst)
        self.h_cur = dst

    def stage_ple(self, l, dst):
        P = self.P
        self.new_stage()
        ps = self.ps
        pg = self.carve([128, 8, D], BF16)
        pp = self.carve([128, 2, D], BF16)
        NS = 4
        hts = [self.carve([128, 8, TT], F32) for _ in range(NS)]
        xns = [self.carve([128, 8, TT], BF16) for _ in range(2)]
        sq = self.carve([128, 8, TT], BF16)
        rs = self.carve([128, TT], F32)
        pts = [self.carve([128, 2, TT], BF16) for _ in range(2)]
        sg = [self.carve([128, TT], F32) for _ in range(2)]
        tm = [self.carve([128, TT], F32) for _ in range(2)]
        self.load_w(pg, self.w["ple_gate_w"][l], "pg", 8)
        self.load_w(pp, self.w["ple_proj_w"][l], "pp", 2)
        pTv = self.pT[l].rearrange("(c p) t -> p c t", p=128)

        def load_pt(slot, ti):
            P.dma("pool", ("ldpt", slot), pts[slot], pTv[:, :, ti * TT:(ti + 1) * TT], writes=[("pt", slot)])
        for t_ in range(min(NS - 1, self.NT)):
            self.load_h(t_ % NS, t_, hts[t_ % NS])
        load_pt(0, 0)
        for ti in range(self.NT):
            slot = ti % NS
            x2 = ti % 2
            ht, pt = hts[slot], pts[x2]
            hkey = ("ht", slot)
            nslot = (ti + 1) % NS
            if ti + NS - 1 < self.NT:
                self.load_h((ti + NS - 1) % NS, ti + NS - 1, hts[(ti + NS - 1) % NS])
            if ti + 1 < self.NT:
                load_pt(1 - x2, ti + 1)
            xn, xkey = xns[x2], ("xn", x2)
            if ti == 0:
                self.norm(ht, hkey, 24, xn, xkey, sq, rs)
            for dc in range(8):
                if dc == 1 and ti + 1 < self.NT:
                    self.norm_a(hts[nslot], ("ht", nslot), sq)
                if dc == 4 and ti + 1 < self.NT:
                    self.norm_b(hts[nslot], ("ht", nslot), 24, xns[1 - x2], ("xn", 1 - x2), sq, rs)
                dsl = slice(dc * 128, (dc + 1) * 128)
                b2 = dc % 2
                for kc in range(8):
                    P.op("pe", lambda e, kc=kc, dsl=dsl, b2=b2, xn=xn: e.matmul(ps[b2], lhsT=pg[:, kc, dsl], rhs=xn[:, kc, :], start=(kc == 0), stop=(kc == 7)),
                         reads=["pg", xkey], writes=["ps%d" % b2], inc=(kc == 7))
                for kc in range(2):
                    P.op("pe", lambda e, kc=kc, dsl=dsl, b2=b2, pt=pt: e.matmul(ps[2 + b2], lhsT=pp[:, kc, dsl], rhs=pt[:, kc, :], start=(kc == 0), stop=(kc == 1)),
                         reads=["pp", ("pt", x2)], writes=["ps%d" % (2 + b2)], inc=(kc == 1))
                P.op("act", lambda e, b2=b2: e.activation(out=sg[b2], in_=ps[b2], func=AF.Sigmoid), reads=["ps%d" % b2], writes=[("sg", b2)])
                P.op("dve", lambda e, b2=b2: e.tensor_tensor(out=tm[b2], in0=ps[2 + b2], in1=sg[b2], op=ALU.mult),
                     reads=["ps%d" % (2 + b2), ("sg", b2)], writes=[("tm", b2)])
                P.op("pool", lambda e, b2=b2, dc=dc, ht=ht: e.tensor_tensor(out=ht[:, dc, :], in0=ht[:, dc, :], in1=tm[b2], op=ALU.add),
                     reads=[("tm", b2), hkey], writes=[hkey])
            self.store_h(slot, ti, ht, dst)
        self.h_cur = dst

    def build(self):
        nc = self.nc
        with ExitStack() as st:
            self.P = P = Prog(nc, st)
            self.BIG32 = 52448
            self.big = st.enter_context(nc.sbuf_tensor("big", [128, self.BIG32], F32))[:]
            cbf_t = st.enter_context(nc.sbuf_tensor("cbf_t", [128, 5 * 128], BF16))
            c32_t = st.enter_context(nc.sbuf_tensor("c32_t", [128, 2 * 128 + 3], F32))
            cols_t = st.enter_context(nc.sbuf_tensor("cols_t", [128, 4 * NCOLS], F32))
            self.psp = [st.enter_context(nc.psum_tensor("psp%d" % i, [128, 1024], F32))[:] for i in range(4)]
            self.ps = [self.psp[i // 2][:, (i % 2) * 512:(i % 2 + 1) * 512] for i in range(8)]
            self.off = 0
            self.ones_bf = cbf_t[:, 0:128]
            self.bd_bf = cbf_t[:, 128:256]
            self.id_bf = cbf_t[:, 256:384]
            self.pm_bf = cbf_t[:, 384:512]
            self.tri_bf = cbf_t[:, 512:640]
            self.tri32 = c32_t[:, 0:128]
            self.ones32 = c32_t[:, 128:256]
            self.eps_col = c32_t[:, 256:257]
            self.invf_col = c32_t[:, 257:258]
            self.negpi_col = c32_t[:, 258:259]
            self.cols_all = cols_t
            P.dma("pool", ("c", 0), cbf_t[:], self.cbf_d[:, 0:640], writes=["cbf"])
            P.dma("sp", ("c", 1), c32_t[:], self.c32_d, writes=["c32"])
            L = self.cols_d.shape[0]
            P.dma_many("sp", ("c", 2), [(cols_t[:, i * NCOLS:(i + 1) * NCOLS], self.cols_d[i]) for i in range(L)], writes=["cols"])
            P.barrier()
            self.h_cur = self.xT
            nl = len(self.layers)
            if self.stages is None:
                self.stage_setup()
            for li, l in enumerate(self.layers):
                self.cols = cols_t[:, l * NCOLS:(l + 1) * NCOLS]
                last = (li == nl - 1)
                stages = self.stages or ["ffn1", "proj", "attn", "merge", "ffn2", "ple"]
                for sname in stages:
                    is_last_stage = last and sname == stages[-1]
                    dst = self.yT if is_last_stage else self.hA
                    if sname == "setup":
                        self.stage_setup()
                    elif sname == "proj":
                        self.stage_proj(l)
                    elif sname == "attn":
                        self.stage_attn(l)
                    elif sname == "merge":
                        self.stage_merge(l, dst)
                    elif sname == "ple":
                        self.stage_ple(l, dst)
                    elif sname == "ffn1":
                        self.stage_ffn(l, self.w["ffn1_wi"], self.w["ffn1_wo"], 0, dst)
                    elif sname == "ffn2":
                        self.stage_ffn(l, self.w["ffn2_wi"], self.w["ffn2_wo"], 16, dst)
            P.barrier()
            P.emit()
        return nc


def make_consts():
    cbf = np.zeros((128, 5 * 128 + 12 * 512), np.float32)
    p = np.arange(128)
    cbf[:, 0:128] = 1.0
    cbf[:, 128:256] = (p[:, None] // 64 == p[None, :] // 64).astype(np.float32)
    cbf[:, 256:384] = np.eye(128, dtype=np.float32)
    pm = np.zeros((128, 128), np.float32)
    for m in range(128):
        r = m % 64
        if r < 8:
            pm[m + 8, m] = -1.0
        elif r < 16:
            pm[m - 8, m] = 1.0
    cbf[:, 384:512] = pm
    cbf[:, 512:640] = (p[:, None] > p[None, :]).astype(np.float32)
    j = np.arange(512)
    for typ in range(3):
        for m in range(4):
            s = p[:, None] + 128 * m
            if typ == 0:
                keep = s < j[None, :]
            elif typ == 1:
                keep = s <= j[None, :]
            else:
                keep = (s // 64) <= (j[None, :] // 64)
            cbf[:, 640 + (typ * 4 + m) * 512: 640 + (typ * 4 + m + 1) * 512] = np.where(keep, 0.0, NEG)
    c32 = np.zeros((128, 2 * 128 + 3), np.float32)
    c32[:, 258] = -math.pi
    c32[:, 0:128] = (p[:, None] > p[None, :]).astype(np.float32)
    c32[:, 128:256] = 1.0
    c32[:, 256] = EPS
    inv_freq = (500000.0 ** (-np.arange(0, 16, 2, dtype=np.float32) / np.float32(16))).astype(np.float32)
    r = p % 64
    c32[:, 257] = np.where(r < 16, inv_freq[r % 8], 0.0)
    return cbf, c32


def make_cols(inp, L):
    cols = np.zeros((L, 128, NCOLS), np.float32)
    for l in range(L):
        cols[l, :, 0:8] = np.asarray(inp["ffn1_norm"][l]).reshape(8, 128).T
        cols[l, :, 8:16] = np.asarray(inp["mix_norm"][l]).reshape(8, 128).T
        cols[l, :, 16:24] = np.asarray(inp["ffn2_norm"][l]).reshape(8, 128).T
        cols[l, :, 24:32] = np.asarray(inp["ple_norm"][l]).reshape(8, 128).T
        cols[l, :, 32] = np.tile(np.asarray(inp["qk_gain_fox"][l][0]), 2)
        cols[l, :, 33] = np.tile(np.asarray(inp["qk_gain_fox"][l][1]), 2)
        cols[l, :, 34] = np.tile(np.asarray(inp["qk_gain_diff"][l][0]), 2)
        cols[l, :, 35] = np.tile(np.asarray(inp["qk_gain_diff"][l][1]), 2)
        cols[l, :, 36] = np.asarray(inp["diff_subln"][l])
        cols[l, 0:4, 37] = np.asarray(inp["b_forget"][l])
    return cols


LAM_INITS = [0.8 - 0.6 * math.exp(-0.3 * i) for i in range(4)]


def prep_shared(inp):
    L = 4
    f = lambda a: np.ascontiguousarray(np.asarray(a), dtype=np.float32)
    cbf, c32 = make_consts()
    sh = {nm: f(inp[nm]) for nm in ["ffn1_wi", "ffn1_wo", "w_in", "w_br", "w_o", "ffn2_wi", "ffn2_wo", "ple_gate_w", "ple_proj_w"]}
    sh["cols"] = make_cols(inp, L)
    sh["dlam"] = np.ascontiguousarray(np.broadcast_to(np.asarray(inp["diff_lambda"], np.float32).reshape(L, 1, 256), (L, 128, 256)))
    sh["cbf"] = cbf
    sh["c32"] = c32
    return sh


def prep_core(inp, b):
    x = np.asarray(inp["x"][b], np.float32)
    p = np.asarray(inp["p"][:, b], np.float32)
    pos = np.asarray(inp["positions"][b], np.int32)
    S = x.shape[0]
    return {
        "xT": np.ascontiguousarray(x.T),
        "pT": np.ascontiguousarray(p.transpose(0, 2, 1)),
        "posb": np.ascontiguousarray(np.broadcast_to(pos[None, :], (128, S))),
    }


_NC_CACHE = {}


def kernel(**inputs):
    B, S, _ = inputs["x"].shape
    key = (S,)
    if key not in _NC_CACHE:
        _NC_CACHE[key] = Builder(S, [0, 1, 2, 3], 4, LAM_INITS).build()
    nc = _NC_CACHE[key]
    sh = prep_shared(inputs)
    in_maps = []
    for b in range(B):
        m = dict(sh)
        m.update(prep_core(inputs, b))
        in_maps.append(m)
    res = run_bass_kernel_spmd(nc, in_maps, core_ids=list(range(B)))
    out = np.stack([np.ascontiguousarray(r["yT"].T) for r in res.results], axis=0)
    return out.astype(np.float32)
```

```python
import math
from contextlib import ExitStack

import numpy as np
import concourse.bass as bass
import concourse.mybir as mybir
from concourse.bass_utils import run_bass_kernel_spmd

F32 = mybir.dt.float32
BF16 = mybir.dt.bfloat16
I32 = mybir.dt.int32
AF = mybir.ActivationFunctionType
ALU = mybir.AluOpType

D = 1024
DFF = 2816
NFC = DFF // 128
PD = 256
EPS = 1e-6
NEG = -30000.0
NCOLS = 40
TT = 512


class Prog:
    ENGS = ("pe", "act", "dve", "pool", "sp")

    def __init__(self, nc, stack):
        self.nc = nc
        self.stack = stack
        self.ops = {e: [] for e in self.ENGS}
        self.sem = {e: stack.enter_context(nc.semaphore("s_" + e)) for e in self.ENGS}
        self.cnt = {e: 0 for e in self.ENGS}
        self.known = {e: {} for e in self.ENGS}
        self.lastw = {}
        self.readers = {}
        self.pending = {e: [] for e in self.ENGS}
        self.dsems = {}
        self.n_ops = 0

    def dsem(self, key):
        d = self.dsems.get(key)
        if d is None:
            s = self.stack.enter_context(self.nc.semaphore("d%d" % len(self.dsems)))
            d = {"sem": s, "cnt": 0}
            self.dsems[key] = d
        return d

    def _need(self, eng, tok, waits):
        if tok is None:
            return
        sem, val, src = tok
        if src == eng and eng == "pe":
            return
        k = self.known[eng]
        if k.get(id(sem), 0) >= val:
            return
        k[id(sem)] = val
        waits.append((sem, val))

    def _deps(self, eng, reads, writes):
        waits = []
        for r in reads:
            self._need(eng, self.lastw.get(r), waits)
        for w in writes:
            self._need(eng, self.lastw.get(w), waits)
            for t in self.readers.get(w, ()):
                self._need(eng, t, waits)
        best = {}
        order = []
        for sem, val in waits:
            if id(sem) not in best:
                order.append(sem)
                best[id(sem)] = val
            else:
                best[id(sem)] = max(best[id(sem)], val)
        return [(s, best[id(s)]) for s in order]

    def _commit(self, tok, reads, writes):
        for w in writes:
            self.lastw[w] = tok
            self.readers[w] = []
        for r in reads:
            self.readers.setdefault(r, []).append(tok)

    def op(self, eng, fn, reads=(), writes=(), inc=True):
        waits = self._deps(eng, reads, writes)
        self.n_ops += 1 + len(waits)
        if inc:
            self.cnt[eng] += 1
            tok = (self.sem[eng], self.cnt[eng], eng)
            self.ops[eng].append((waits, fn, (self.sem[eng], 1)))
            for (r, w) in self.pending[eng]:
                self._commit(tok, r, w)
            self.pending[eng] = []
            self._commit(tok, reads, writes)
        else:
            self.ops[eng].append((waits, fn, None))
            self.pending[eng].append((tuple(reads), tuple(writes)))

    def dma(self, eng, skey, out, in_, reads=(), writes=()):
        self.dma_many(eng, skey, [(out, in_)], reads, writes)

    def dma_many(self, eng, skey, pairs, reads=(), writes=()):
        d = self.dsem(skey)
        waits = self._deps(eng, reads, writes)
        self.n_ops += len(pairs) + len(waits)
        first = True
        for (o, i) in pairs:
            d["cnt"] += 16
            self.ops[eng].append((waits if first else [], (lambda e, o=o, i=i: e.dma_start(out=o, in_=i)), (d["sem"], 16)))
            first = False
        tok = (d["sem"], d["cnt"], "dma")
        self._commit(tok, reads, writes)

    def barrier(self):
        assert all(not p for p in self.pending.values())
        for e in self.ENGS:
            waits = []
            for o in self.ENGS:
                if o != e and self.cnt[o] > 0:
                    self._need(e, (self.sem[o], self.cnt[o], o), waits)
            for d in self.dsems.values():
                if d["cnt"] > 0:
                    self._need(e, (d["sem"], d["cnt"], "dma"), waits)
            self.ops[e].append((waits, None, None))
        self.lastw = {}
        self.readers = {}

    def emit(self):
        nc = self.nc
        with nc.Block() as block:
            def run(name):
                def body(e):
                    for waits, fn, inc in self.ops[name]:
                        for sem, val in waits:
                            e.wait_ge(sem, val)
                        if fn is not None:
                            ins = fn(e)
                            if inc is not None:
                                ins.then_inc(inc[0], inc[1])
                return body
            block.tensor(run("pe"))
            block.scalar(run("act"))
            block.vector(run("dve"))
            block.gpsimd(run("pool"))
            block.sync(run("sp"))


class Builder:
    def __init__(self, S, layers, n_layers_total, lam_inits, stages=None, debug=False):
        self.S = S
        self.NT = S // TT
        self.NB = S // 128
        self.layers = layers
        self.stages = stages
        self.debug = debug
        self.lam_inits = lam_inits
        L = n_layers_total
        nc = bass.Bass("TRN2", target_bir_lowering=False)
        self.nc = nc
        dt = nc.dram_tensor
        self.xT = dt("xT", [D, S], F32, kind="ExternalInput").ap()
        self.pT = dt("pT", [L, PD, S], F32, kind="ExternalInput").ap()
        self.posb = dt("posb", [128, S], I32, kind="ExternalInput").ap()
        self.w = {}
        for nm, shp in [("ffn1_wi", [L, D, 2 * DFF]), ("ffn1_wo", [L, DFF, D]), ("w_in", [L, D, 6148]),
                        ("w_br", [L, D, D]), ("w_o", [L, D, D]), ("ffn2_wi", [L, D, 2 * DFF]),
                        ("ffn2_wo", [L, DFF, D]), ("ple_gate_w", [L, D, D]), ("ple_proj_w", [L, PD, D])]:
            self.w[nm] = dt(nm, shp, F32, kind="ExternalInput").ap()
        self.cols_d = dt("cols", [L, 128, NCOLS], F32, kind="ExternalInput").ap()
        self.dlam_d = dt("dlam", [L, 128, 256], F32, kind="ExternalInput").ap()
        self.cbf_d = dt("cbf", [128, 5 * 128 + 12 * 512], F32, kind="ExternalInput").ap()
        self.c32_d = dt("c32", [128, 2 * 128 + 3], F32, kind="ExternalInput").ap()
        self.yT = dt("yT", [D, S], F32, kind="ExternalOutput").ap()
        kd = "ExternalOutput" if debug else "Internal"
        self.hA = dt("hA", [D, S], F32).ap()
        self.qk = dt("qk", [2048, S], BF16, kind=kd).ap()
        self.augq = dt("augq", [4, 6, S], BF16, kind=kd).ap()
        self.augk = dt("augk", [4, 6, S], BF16, kind=kd).ap()
        self.vtok = dt("vtok", [S, D], BF16, kind=kd).ap()
        self.OT = dt("OT", [D, S], BF16, kind=kd).ap()
        self.ctd = dt("ctd", [128, S], F32, kind=kd).ap()
        self.std = dt("std", [128, S], F32, kind=kd).ap()
        self.dbg = {}

    def carve(self, shape, dtype):
        n = 1
        for s in shape[1:]:
            n *= s
        nbytes = n * (2 if dtype == BF16 else 4)
        n32 = (nbytes + 3) // 4
        n32 = (n32 + 7) // 8 * 8
        off = self.off
        self.off += n32
        assert self.off <= self.BIG32, ("SBUF overflow", self.off, self.BIG32)
        ap = self.big[:, off:off + n32]
        if dtype == BF16:
            ap = ap.bitcast(BF16)[:, 0:n]
        elif dtype == I32:
            ap = ap.bitcast(I32)[:, 0:n]
        else:
            ap = ap[:, 0:n]
        if len(shape) == 3:
            ap = ap.rearrange("p (a b) -> p a b", a=shape[1])
        elif len(shape) == 4:
            ap = ap.rearrange("p (a b c) -> p a b c", a=shape[1], b=shape[2])
        return ap

    def new_stage(self):
        self.P.barrier()
        self.off = 0

    def load_w(self, dst, src3, key, nsplit):
        P = self.P
        kcs = dst.shape[1]
        pairs = []
        for kc in range(kcs):
            pairs.append((dst[:, kc, :], src3[kc * 128:(kc + 1) * 128, :]))
        P.dma_many("pool", ("w", key), pairs, writes=[key])

    def norm_a(self, ht, hkey, sq):
        n = sq.shape[1]
        self.P.op("act", lambda e: e.activation(out=sq, in_=ht[:, 0:n, :], func=AF.Square), reads=[hkey], writes=["sq"])

    def norm_b(self, ht, hkey, gcol, xn, xkey, sq, rs):
        P = self.P
        ss = self.ps[7]
        cols = self.cols
        n = sq.shape[1]
        for c in range(8):
            if c > 0 and c % n == 0:
                P.op("act", lambda e, c=c: e.activation(out=sq, in_=ht[:, c:c + n, :], func=AF.Square), reads=[hkey], writes=["sq"])
            P.op("pe", lambda e, c=c: e.matmul(ss, lhsT=self.ones_bf, rhs=sq[:, c % n, :], start=(c == 0), stop=(c == 7)),
                 reads=["sq"], writes=["ps7"], inc=(c % n == n - 1))
        P.op("act", lambda e: e.activation(out=rs, in_=ss, func=AF.Ln, bias=self.eps_col, scale=1.0 / D),
             reads=["ps7"], writes=["rs"])
        P.op("act", lambda e: e.activation(out=rs, in_=rs, func=AF.Exp, scale=-0.5), reads=["rs"], writes=["rs"])
        for c in range(8):
            P.op("dve", lambda e, c=c: e.scalar_tensor_tensor(out=xn[:, c, :], in0=ht[:, c, :], scalar=cols[:, gcol + c:gcol + c + 1],
                                                             in1=rs, op0=ALU.mult, op1=ALU.mult),
                 reads=[hkey, "rs"], writes=[xkey])

    def norm(self, ht, hkey, gcol, xn, xkey, sq, rs):
        self.norm_a(ht, hkey, sq)
        self.norm_b(ht, hkey, gcol, xn, xkey, sq, rs)

    def h_src(self):
        return self.h_cur

    def load_h(self, slot, ti, ht):
        P = self.P
        src = self.h_cur.rearrange("(c p) t -> p c t", p=128)[:, :, ti * TT:(ti + 1) * TT]
        P.dma("sp", ("ldh", slot), ht, src, reads=[("hd", ti)], writes=[("ht", slot)])

    def store_h(self, slot, ti, ht, dst):
        P = self.P
        d = dst.rearrange("(c p) t -> p c t", p=128)[:, :, ti * TT:(ti + 1) * TT]
        P.dma("sp", ("sth", slot), d, ht, reads=[("ht", slot)], writes=[("hd", ti)])

    def stage_ffn(self, l, wi_d, wo_d, gcol, dst):
        P = self.P
        self.new_stage()
        wi = self.carve([128, 8, 2 * DFF], BF16)
        wo = self.carve([128, NFC, D], BF16)
        hts = [self.carve([128, 8, TT], F32) for _ in range(2)]
        xn = self.carve([128, 8, TT], BF16)
        hm = self.carve([128, NFC, TT], BF16)
        sq = self.carve([128, 4, TT], BF16)
        sa = self.carve([128, TT], F32)
        rs = self.carve([128, TT], F32)
        wsrc = wi_d[l].rearrange("(kc p) n -> p kc n", p=128)
        for G in range(NFC // 2):
            c0 = 256 * G
            P.dma_many("pool", ("w", "wi", G), [(wi[:, :, c0:c0 + 256], wsrc[:, :, c0:c0 + 256]),
                                                (wi[:, :, DFF + c0:DFF + c0 + 256], wsrc[:, :, DFF + c0:DFF + c0 + 256])], writes=[("wi", G)])
        for G in range((NFC + 3) // 4):
            P.dma_many("pool", ("w", "wo", G), [(wo[:, fc, :], wo_d[l][fc * 128:(fc + 1) * 128, :]) for fc in range(4 * G, min(4 * G + 4, NFC))],
                       writes=[("wo", G)])
        ps = self.ps
        self.load_h(0, 0, hts[0])
        for ti in range(self.NT):
            slot = ti % 2
            ht = hts[slot]
            hkey = ("ht", slot)
            if ti + 1 < self.NT:
                self.load_h(1 - slot, ti + 1, hts[1 - slot])
            if ti == 0:
                self.norm(ht, hkey, gcol, xn, "xn", sq, rs)
            for fc in range(NFC):
                pa = ps[(fc % 2) * 2]
                pg = ps[(fc % 2) * 2 + 1]
                ka, kg = "ps%d" % ((fc % 2) * 2), "ps%d" % ((fc % 2) * 2 + 1)
                for kc in range(8):
                    P.op("pe", lambda e, kc=kc, fc=fc, pa=pa: e.matmul(pa, lhsT=wi[:, kc, fc * 128:(fc + 1) * 128], rhs=xn[:, kc, :],
                                                                         start=(kc == 0), stop=(kc == 7)),
                         reads=[("wi", fc // 2), "xn"], writes=[ka], inc=(kc == 7))
                for kc in range(8):
                    P.op("pe", lambda e, kc=kc, fc=fc, pg=pg: e.matmul(pg, lhsT=wi[:, kc, DFF + fc * 128:DFF + (fc + 1) * 128], rhs=xn[:, kc, :],
                                                                         start=(kc == 0), stop=(kc == 7)),
                         reads=[("wi", fc // 2), "xn"], writes=[kg], inc=(kc == 7))
                P.op("act", lambda e, pa=pa: e.activation(out=sa, in_=pa, func=AF.Silu), reads=[ka], writes=["sa"])
                P.op("dve", lambda e, pg=pg, fc=fc: e.tensor_tensor(out=hm[:, fc, :], in0=pg, in1=sa, op=ALU.mult),
                     reads=[kg, "sa"], writes=["hm"])
            for dc in range(8):
                if dc == 0 and ti + 1 < self.NT:
                    self.norm_a(hts[1 - slot], ("ht", 1 - slot), sq)
                if dc == 3 and ti + 1 < self.NT:
                    self.norm_b(hts[1 - slot], ("ht", 1 - slot), gcol, xn, "xn", sq, rs)
                po = ps[4 + dc % 2]
                ko = "ps%d" % (4 + dc % 2)
                for fc in range(NFC):
                    P.op("pe", lambda e, fc=fc, dc=dc, po=po: e.matmul(po, lhsT=wo[:, fc, dc * 128:(dc + 1) * 128], rhs=hm[:, fc, :],
                                                                         start=(fc == 0), stop=(fc == NFC - 1)),
                         reads=[("wo", fc // 4), "hm"], writes=[ko], inc=(fc == NFC - 1))
                P.op("dve", lambda e, dc=dc, po=po, ht=ht: e.scalar_tensor_tensor(out=ht[:, dc, :], in0=po, scalar=0.5, in1=ht[:, dc, :],
                                                                                 op0=ALU.mult, op1=ALU.add),
                     reads=[ko, hkey], writes=[hkey])
            self.store_h(slot, ti, ht, dst)
        self.h_cur = dst


    def stage_setup(self):
        P = self.P
        self.new_stage()
        S = self.S
        posi = self.carve([128, S], I32)
        ang = self.carve([128, S], F32)
        tmp = self.carve([128, S], F32)
        tmp2 = self.carve([128, S], F32)
        onesr = self.carve([128, S], BF16)
        P.dma("sp", ("ld", "pos"), posi, self.posb, writes=["posi"])
        P.op("dve", lambda e: e.tensor_copy(out=ang, in_=posi), reads=["posi"], writes=["ang"])
        P.op("dve", lambda e: e.tensor_scalar(out=ang, in0=ang, scalar1=self.invf_col, scalar2=None, op0=ALU.mult),
             reads=["ang"], writes=["ang"])
        two_pi = 2.0 * math.pi
        C1 = 6.28125
        C2 = two_pi - C1
        ki = posi

        def sincos(dst, dkey, shift, stkey, dram):
            if shift != 0.0:
                P.op("dve", lambda e: e.tensor_scalar(out=tmp, in0=ang, scalar1=shift, scalar2=None, op0=ALU.add), reads=["ang"], writes=["tmp"])
                srcang = tmp
            else:
                P.op("dve", lambda e: e.tensor_copy(out=tmp, in_=ang), reads=["ang"], writes=["tmp"])
                srcang = tmp
            P.op("dve", lambda e: e.tensor_scalar(out=dst, in0=srcang, scalar1=1.0 / two_pi, scalar2=None, op0=ALU.mult), reads=["tmp"], writes=[dkey])
            P.op("dve", lambda e: e.tensor_copy(out=ki, in_=dst), reads=[dkey], writes=["posi"])
            P.op("dve", lambda e: e.tensor_copy(out=dst, in_=ki), reads=["posi"], writes=[dkey])
            P.op("dve", lambda e: e.scalar_tensor_tensor(out=tmp, in0=dst, scalar=-C1, in1=tmp, op0=ALU.mult, op1=ALU.add), reads=[dkey, "tmp"], writes=["tmp"])
            P.op("dve", lambda e: e.scalar_tensor_tensor(out=tmp, in0=dst, scalar=-C2, in1=tmp, op0=ALU.mult, op1=ALU.add), reads=[dkey, "tmp"], writes=["tmp"])
            P.op("dve", lambda e: e.tensor_scalar(out=dst, in0=tmp, scalar1=math.pi, scalar2=two_pi, op0=ALU.is_gt, op1=ALU.mult), reads=["tmp"], writes=[dkey])
            P.op("dve", lambda e: e.tensor_tensor(out=tmp, in0=tmp, in1=dst, op=ALU.subtract), reads=["tmp", dkey], writes=["tmp"])
            P.op("dve", lambda e: e.tensor_scalar(out=dst, in0=tmp, scalar1=-math.pi, scalar2=two_pi, op0=ALU.is_lt, op1=ALU.mult), reads=["tmp"], writes=[dkey])
            P.op("dve", lambda e: e.tensor_tensor(out=tmp, in0=tmp, in1=dst, op=ALU.add), reads=["tmp", dkey], writes=["tmp"])
            P.op("act", lambda e: e.activation(out=dst, in_=tmp, func=AF.Sin), reads=["tmp"], writes=[dkey])
            P.dma("sp", ("st", stkey), dram, dst, reads=[dkey], writes=[stkey])

        sincos(tmp2, "tmp2", 0.0, "std", self.std)
        sincos(tmp2, "tmp2", 0.5 * math.pi, "ctd", self.ctd)
        P.op("pool", lambda e: e.memset(onesr, 1.0), writes=["onesr"])
        pairs = []
        for i in range(3):
            pairs.append((self.augk[:, i, :], onesr[0:4, :]))
            pairs.append((self.augq[:, 3 + i, :], onesr[0:4, :]))
        P.dma_many("sp", ("st", "ones"), pairs, reads=["onesr"], writes=["augones"])

    def stage_proj(self, l):
        P = self.P
        self.new_stage()
        S = self.S
        cols = self.cols
        w_in = self.w["w_in"][l].rearrange("(kc p) n -> p kc n", p=128)
        wqk = self.carve([128, 8, 2048], BF16)
        wv = self.carve([128, 8, 1024], BF16)
        wf = self.carve([128, 8, 4], BF16)
        nlf = self.carve([128, S], F32)
        off_mark = self.off
        hts = [self.carve([128, 8, TT], F32) for _ in range(2)]
        xns = [self.carve([128, 8, TT], BF16) for _ in range(2)]
        sq = self.carve([128, 8, TT], BF16)
        rs = self.carve([128, TT], F32)
        cts = [self.carve([128, TT], F32) for _ in range(2)]
        sts = [self.carve([128, TT], F32) for _ in range(2)]
        sqb = [self.carve([128, TT], BF16) for _ in range(3)]
        rs2 = [self.carve([128, TT], F32) for _ in range(3)]
        qn32 = [self.carve([128, TT], F32) for _ in range(3)]
        qnb = [self.carve([128, TT], BF16) for _ in range(3)]
        t1 = [self.carve([128, TT], F32) for _ in range(3)]
        t2 = [self.carve([128, TT], F32) for _ in range(3)]
        outq = [self.carve([128, TT], BF16) for _ in range(6)]
        vout = [self.carve([128, 1024], BF16) for _ in range(2)]
        gsc = self.carve([128, 4], F32)
        negb = self.carve([128, 1], F32)
        e4 = self.carve([128, TT], F32)
        pairs = []
        for (d0, s0, wd) in [(0, 0, 256), (256, 256, 256), (512, 768, 256), (768, 1024, 256), (1024, 1540, 512), (1536, 2052, 512)]:
            pairs.append((wqk[:, :, d0:d0 + wd], w_in[:, :, s0:s0 + wd]))
        P.dma_many("pool", ("w", "wqk"), pairs, writes=["wqk"])
        pairs = []
        for (d0, s0, wd) in [(0, 512, 256), (256, 1280, 256), (512, 2564, 512)]:
            pairs.append((wv[:, :, d0:d0 + wd], w_in[:, :, s0:s0 + wd]))
        pairs.append((wf, w_in[:, :, 1536:1540]))
        P.dma_many("pool", ("w", "wv"), pairs, writes=["wv", "wf"])
        P.op("dve", lambda e: e.tensor_scalar(out=gsc[:, 0:1], in0=cols[:, 32:33], scalar1=0.125, scalar2=None, op0=ALU.mult), writes=["gsc"])
        P.op("dve", lambda e: e.tensor_copy(out=gsc[:, 1:2], in_=cols[:, 33:34]), writes=["gsc"])
        P.op("dve", lambda e: e.tensor_scalar(out=gsc[:, 2:3], in0=cols[:, 34:35], scalar1=0.125, scalar2=None, op0=ALU.mult), writes=["gsc"])
        P.op("dve", lambda e: e.tensor_copy(out=gsc[:, 3:4], in_=cols[:, 35:36]), writes=["gsc"])
        P.op("dve", lambda e: e.tensor_scalar(out=negb, in0=cols[:, 37:38], scalar1=-1.0, scalar2=None, op0=ALU.mult), writes=["negb"])
        ps = self.ps
        self.load_h(0, 0, hts[0])
        oslot = 0
        for ti in range(self.NT):
            slot = ti % 2
            ht = hts[slot]
            hkey = ("ht", slot)
            tsl = slice(ti * TT, (ti + 1) * TT)
            ct, stt = cts[slot], sts[slot]
            P.dma("sp", ("ldc", slot), ct, self.ctd[:, tsl], reads=["ctd"], writes=[("ct", slot)])
            P.dma("sp", ("lds", slot), stt, self.std[:, tsl], reads=["std"], writes=[("st", slot)])
            if ti + 1 < self.NT:
                self.load_h(1 - slot, ti + 1, hts[1 - slot])
            xn, xkey = xns[slot], ("xn", slot)
            if ti == 0:
                self.norm(ht, hkey, 8, xn, xkey, sq, rs)
            def MM(idx):
                pq, kq = ps[idx % 4], "ps%d" % (idx % 4)
                for kc in range(8):
                    P.op("pe", lambda e, kc=kc, xn=xn: e.matmul(pq, lhsT=wqk[:, kc, idx * 128:(idx + 1) * 128], rhs=xn[:, kc, :],
                                                        start=(kc == 0), stop=(kc == 7)),
                         reads=["wqk", xkey], writes=[kq], inc=(kc == 7))

            def POST1(idx, oq, okey, osl):
                pq, kq = ps[idx % 4], "ps%d" % (idx % 4)
                b3 = idx % 3
                if idx < 4:
                    sc = 0.125 if idx < 2 else 1.0
                    P.op("act", lambda e: e.activation(out=oq, in_=pq, func=AF.Copy, scale=sc), reads=[kq], writes=[okey])
                    P.dma("sp", ("stq", osl), self.qk[idx * 128:(idx + 1) * 128, tsl], oq, reads=[okey], writes=[("qkd", idx, ti)])
                    return
                gi = (0 if idx < 6 else 1) if idx < 8 else (2 if idx < 12 else 3)
                pss, kss = ps[4], "ps4"
                P.op("act", lambda e: e.activation(out=sqb[b3], in_=pq, func=AF.Square), reads=[kq], writes=[("sqb", b3)])
                P.op("pe", lambda e: e.matmul(pss, lhsT=self.bd_bf, rhs=sqb[b3], start=True, stop=True), reads=[("sqb", b3)], writes=[kss])
                P.op("act", lambda e: e.activation(out=rs2[b3], in_=pss, func=AF.Ln, bias=self.eps_col, scale=1.0 / 64), reads=[kss], writes=[("rs2", b3)])
                P.op("act", lambda e: e.activation(out=rs2[b3], in_=rs2[b3], func=AF.Exp, scale=-0.5), reads=[("rs2", b3)], writes=[("rs2", b3)])
                if idx < 8:
                    P.op("dve", lambda e: e.scalar_tensor_tensor(out=oq, in0=pq, scalar=gsc[:, gi:gi + 1], in1=rs2[b3], op0=ALU.mult, op1=ALU.mult),
                         reads=[kq, ("rs2", b3), "gsc"], writes=[okey])
                    P.dma("sp", ("stq", osl), self.qk[idx * 128:(idx + 1) * 128, tsl], oq, reads=[okey], writes=[("qkd", idx, ti)])
                else:
                    P.op("dve", lambda e: e.scalar_tensor_tensor(out=qn32[b3], in0=pq, scalar=gsc[:, gi:gi + 1], in1=rs2[b3], op0=ALU.mult, op1=ALU.mult),
                         reads=[kq, ("rs2", b3), "gsc"], writes=[("qn32", b3)])
                    P.op("act", lambda e: e.activation(out=qnb[b3], in_=qn32[b3], func=AF.Copy), reads=[("qn32", b3)], writes=[("qnb", b3)])

            def POST2(idx, oq, okey, osl):
                if idx < 8:
                    return
                b3 = idx % 3
                ppm, kpm = ps[5], "ps5"
                P.op("pe", lambda e: e.matmul(ppm, lhsT=self.pm_bf, rhs=qnb[b3], start=True, stop=True), reads=[("qnb", b3)], writes=[kpm])
                P.op("pool", lambda e, ct=ct: e.tensor_tensor(out=t1[b3], in0=qn32[b3], in1=ct, op=ALU.mult), reads=[("qn32", b3), ("ct", slot)], writes=[("t1", b3)])
                P.op("dve", lambda e, stt=stt: e.tensor_tensor(out=t2[b3], in0=ppm, in1=stt, op=ALU.mult), reads=[kpm, ("st", slot)], writes=[("t2", b3)])
                P.op("pool", lambda e: e.tensor_tensor(out=oq, in0=t1[b3], in1=t2[b3], op=ALU.add), reads=[("t1", b3), ("t2", b3)], writes=[okey])
                P.dma("sp", ("stq", osl), self.qk[idx * 128:(idx + 1) * 128, tsl], oq, reads=[okey], writes=[("qkd", idx, ti)])

            oinfo = {}
            for step in range(16 + 2):
                if step == 6 and ti + 1 < self.NT:
                    self.norm_a(hts[1 - slot], ("ht", 1 - slot), sq)
                if step == 9 and ti + 1 < self.NT:
                    self.norm_b(hts[1 - slot], ("ht", 1 - slot), 8, xns[1 - slot], ("xn", 1 - slot), sq, rs)
                if step < 16:
                    MM(step)
                if 1 <= step <= 16:
                    i1 = step - 1
                    osl = oslot % 6
                    oslot += 1
                    oinfo[i1] = (outq[osl], ("oq", osl), osl)
                    POST1(i1, *oinfo[i1])
                if step >= 2:
                    i2 = step - 2
                    POST2(i2, *oinfo[i2])
            for sub in range(4):
                vs = (ti * 4 + sub) % 2
                vo = vout[vs]
                for half in range(2):
                    vb_ = 6 + (sub * 2 + half) % 2
                    pv, kv = ps[vb_], "ps%d" % vb_
                    for kc in range(8):
                        P.op("pe", lambda e, kc=kc, sub=sub, half=half, pv=pv, xn=xn: e.matmul(pv, lhsT=xn[:, kc, sub * 128:(sub + 1) * 128],
                                                                                       rhs=wv[:, kc, half * 512:(half + 1) * 512],
                                                                                       start=(kc == 0), stop=(kc == 7)),
                             reads=["wv", xkey], writes=[kv], inc=(kc == 7))
                    if half == 0:
                        P.op("act", lambda e, pv=pv, vo=vo: e.activation(out=vo[:, 0:512], in_=pv, func=AF.Copy), reads=[kv], writes=[("vo", vs)])
                    else:
                        P.op("dve", lambda e, pv=pv, vo=vo: e.tensor_copy(out=vo[:, 512:1024], in_=pv), reads=[kv], writes=[("vo", vs)])
                r0 = ti * TT + sub * 128
                P.dma("sp", ("stv", vs), self.vtok[r0:r0 + 128, :], vo, reads=[("vo", vs)], writes=[("vd", ti, sub)])
            pf = ps[7]
            for kc in range(8):
                P.op("pe", lambda e, kc=kc, xn=xn: e.matmul(pf[0:4, :], lhsT=wf[:, kc, :], rhs=xn[:, kc, :], start=(kc == 0), stop=(kc == 7)),
                     reads=["wf", xkey], writes=["ps7"], inc=(kc == 7))
            P.op("act", lambda e: e.activation(out=e4[0:4, :], in_=pf[0:4, :], func=AF.Exp, bias=negb[0:4, :], scale=-1.0),
                 reads=["ps7", "negb"], writes=["e4"])
            P.op("act", lambda e, tsl=tsl: e.activation(out=nlf[0:4, tsl], in_=e4[0:4, :], func=AF.Ln, bias=1.0, scale=1.0),
                 reads=["e4"], writes=["nlf"])
        P.barrier()
        self.off = off_mark
        cumn = self.carve([128, S], F32)
        pcs = [self.carve([128, S], BF16) for _ in range(3)]
        npcs = [self.carve([128, S], BF16) for _ in range(3)]
        p32 = self.carve([128, S], F32)
        A4 = slice(0, 4)
        P.op("dve", lambda e: e.tensor_tensor_scan(out=cumn[A4, :], data0=nlf[A4, :], data1=nlf[A4, :], initial=0.0, op0=ALU.add, op1=ALU.bypass),
             reads=["nlf"], writes=["cumn"])
        for i in range(3):
            P.op("dve", lambda e, i=i: e.tensor_copy(out=pcs[i][A4, :], in_=cumn[A4, :]), reads=["cumn"], writes=[("pc", i)])
            P.op("dve", lambda e, i=i: e.tensor_scalar(out=npcs[i][A4, :], in0=pcs[i][A4, :], scalar1=-1.0, scalar2=None, op0=ALU.mult),
                 reads=[("pc", i)], writes=[("npc", i)])
            if i < 2:
                P.op("dve", lambda e, i=i: e.tensor_copy(out=p32[A4, :], in_=pcs[i][A4, :]), reads=[("pc", i)], writes=["p32"])
                P.op("dve", lambda e: e.tensor_tensor(out=cumn[A4, :], in0=cumn[A4, :], in1=p32[A4, :], op=ALU.subtract),
                     reads=["cumn", "p32"], writes=["cumn"])
        pairs = []
        for i in range(3):
            pairs.append((self.augk[:, 3 + i, :], pcs[i][A4, :]))
            pairs.append((self.augq[:, i, :], npcs[i][A4, :]))
        P.dma_many("sp", ("st", "aug"), pairs, reads=[("pc", 0), ("pc", 1), ("pc", 2), ("npc", 0), ("npc", 1), ("npc", 2)], writes=["augd"])


    def stage_attn(self, l):
        P = self.P
        self.new_stage()
        S, NB, NT = self.S, self.NB, self.NT
        cols = self.cols
        lam_init = self.lam_inits[l]
        ps = self.ps
        masks = self.carve([128, 12, 512], BF16)
        dl = self.carve([128, 256], F32)
        prod = self.carve([128, 64], F32)
        s12 = self.carve([128, 2], F32)
        e12 = self.carve([128, 2], F32)
        neglam = self.carve([128, 1], F32)
        gsub = self.carve([128, 1], F32)
        qb = [self.carve([128, S], BF16) for _ in range(2)]
        kb = [self.carve([128, S], BF16) for _ in range(2)]
        vaug = [self.carve([128, NB, 128], BF16) for _ in range(2)]
        vdf = [self.carve([128, NB, 128], BF16) for _ in range(2)]
        e32 = [self.carve([128, 1024], F32) for _ in range(2)]
        spb = [self.carve([128, 1024], BF16) for _ in range(3)]
        u = [self.carve([128, 1024], F32) for _ in range(2)]
        w = [self.carve([128, 1024], F32) for _ in range(2)]
        A2 = [self.carve([128, 1024], BF16) for _ in range(2)]
        A4 = [[self.carve([128, 1024], BF16) for _ in range(2)] for _ in range(2)]
        A2x = self.carve([128, 1024], BF16)
        accb = [self.carve([128, 512], BF16) for _ in range(3)]
        accm = [self.carve([128, 512], BF16) for _ in range(2)]
        rinv = self.carve([128, 512], F32)
        a1 = self.carve([128, 512], F32)
        a2 = self.carve([128, 512], F32)
        oc = self.carve([128, 512], F32)
        sqo = self.carve([128, 512], BF16)
        rso = self.carve([128, 512], F32)
        obuf = [self.carve([128, 512], BF16) for _ in range(2)]
        cpy = [[self.carve([128, 512], F32) for _ in range(4)] for _ in range(2)]
        P.dma_many("pool", ("w", "masks"), [(masks[:, i, :], self.cbf_d[:, 640 + i * 512:640 + (i + 1) * 512]) for i in range(12)], writes=["masks"])
        P.dma("sp", ("ld", "dl"), dl, self.dlam_d[l], writes=["dl"])
        for sl in range(2):
            P.op("pool", lambda e, sl=sl: e.memset(vaug[sl][:, :, 64:128], 1.0), writes=[("vones", sl)])
        P.op("dve", lambda e: e.tensor_tensor(out=prod, in0=dl[:, 0:64], in1=dl[:, 64:128], op=ALU.mult), reads=["dl"], writes=["prod"])
        P.op("dve", lambda e: e.tensor_reduce(out=s12[:, 0:1], in_=prod, axis=mybir.AxisListType.X, op=ALU.add), reads=["prod"], writes=["s12"])
        P.op("dve", lambda e: e.tensor_tensor(out=prod, in0=dl[:, 128:192], in1=dl[:, 192:256], op=ALU.mult), reads=["dl", "s12"], writes=["prod"])
        P.op("dve", lambda e: e.tensor_reduce(out=s12[:, 1:2], in_=prod, axis=mybir.AxisListType.X, op=ALU.add), reads=["prod"], writes=["s12"])
        P.op("act", lambda e: e.activation(out=e12, in_=s12, func=AF.Exp), reads=["s12"], writes=["e12"])
        P.op("dve", lambda e: e.tensor_tensor(out=neglam, in0=e12[:, 1:2], in1=e12[:, 0:1], op=ALU.subtract), reads=["e12"], writes=["neglam"])
        P.op("dve", lambda e: e.tensor_scalar(out=neglam, in0=neglam, scalar1=-lam_init, scalar2=None, op0=ALU.add), reads=["neglam"], writes=["neglam"])
        P.op("dve", lambda e: e.tensor_scalar(out=gsub, in0=cols[:, 36:37], scalar1=1.0 - lam_init, scalar2=None, op0=ALU.mult), writes=["gsub"])

        heads = [("sb", h) for h in range(4)] + [("fox", h) for h in range(4)] + [("diff", h) for h in range(4)]
        vt = self.vtok.rearrange("(b p) c -> p b c", p=128)
        qk = self.qk

        def load_head(n):
            typ, h = heads[n]
            sl = n % 2
            pairs = []
            hh = (h % 2) * 64
            nvs = max(1, NB // 8)

            def vpairs(dst, c0, wd):
                for j in range(nvs):
                    b0, b1 = j * NB // nvs, (j + 1) * NB // nvs
                    pairs.append((dst[:, b0:b1, 0:wd], vt[:, b0:b1, c0:c0 + wd]))
            if typ == "sb":
                r = (h // 2) * 128 + hh
                pairs.append((qb[sl][0:64, :], qk[r:r + 64, :]))
                r = (2 + h // 2) * 128 + hh
                pairs.append((kb[sl][0:64, :], qk[r:r + 64, :]))
                vpairs(vaug[sl], h * 64, 64)
            elif typ == "fox":
                r = (4 + h // 2) * 128 + hh
                pairs.append((qb[sl][0:64, :], qk[r:r + 64, :]))
                r = (6 + h // 2) * 128 + hh
                pairs.append((kb[sl][0:64, :], qk[r:r + 64, :]))
                pairs.append((qb[sl][64:70, :], self.augq[h]))
                pairs.append((kb[sl][64:70, :], self.augk[h]))
                vpairs(vaug[sl], 256 + h * 64, 64)
            else:
                r = (8 + h) * 128
                pairs.append((qb[sl], qk[r:r + 128, :]))
                r = (12 + h) * 128
                pairs.append((kb[sl], qk[r:r + 128, :]))
                vpairs(vdf[sl], 512 + h * 128, 128)
            wr = [("hq", sl), ("hv" if typ != "diff" else "hvd", sl)]
            P.dma_many("sp", ("ldhd", sl), pairs, reads=[("vones", sl)], writes=wr)

        ostate = {"n": 0}
        psp = self.psp
        ZP = [(0, 1), (6, 7)]
        zpair = [psp[0], psp[3]]

        def score(sl, pr, typ, qi, kbi, zb):
            t0 = qi * TT
            zp, zk = ps[zb], "ps%d" % zb
            diag = kbi >= 4 * qi
            P.op("pe", lambda e: e.matmul(zp, lhsT=kb[sl][pr, kbi * 128:(kbi + 1) * 128], rhs=qb[sl][pr, t0:t0 + TT], start=True, stop=(not diag)),
                 reads=[("hq", sl)], writes=[zk], inc=(not diag))
            if diag:
                m = kbi - 4 * qi
                P.op("pe", lambda e: e.matmul(zp, lhsT=self.id_bf, rhs=masks[:, typ * 4 + m, :], start=False, stop=True),
                     reads=["masks"], writes=[zk])

        def softmax_pass(sl, pr, typ, qi, vfn, targets):
            nkb = 4 * qi + 4
            npair = nkb // 2
            for jp in range(npair + 1):
                if jp < npair:
                    pp = jp % 2
                    for half in range(2):
                        score(sl, pr, typ, qi, 2 * jp + half, ZP[pp][half])
                if jp >= 1:
                    j = jp - 1
                    pp = j % 2
                    zkeys = ["ps%d" % ZP[pp][0], "ps%d" % ZP[pp][1]]
                    P.op("act", lambda e, pp=pp: e.activation(out=A2[pp], in_=zpair[pp], func=AF.Exp), reads=zkeys, writes=[("A", pp)])
                    for half in range(2):
                        blk = 2 * j + half
                        for ti_, (tp, tk, lf) in enumerate(targets):
                            lastt = (ti_ == len(targets) - 1)
                            P.op("pe", lambda e, tp=tp, lf=lf, blk=blk, pp=pp, half=half: e.matmul(tp, lhsT=lf(blk), rhs=A2[pp][:, half * TT:(half + 1) * TT],
                                                                                                 start=(blk == 0), stop=(blk == nkb - 1)),
                                 reads=[("A", pp), vfn], writes=[tk], inc=(blk == nkb - 1 and lastt))

        def sb_head(sl, h):
            pr = slice(0, 64)
            TB = (4, 5)
            tailp = psp[2]
            items = [(qi, j) for qi in range(NT) for j in range((4 * qi + 4) // 2)]
            F = len(items)

            def info(f):
                qi, j = items[f]
                nkb = 4 * qi + 4
                return qi, j, nkb, nkb // 2

            def stA(f):
                qi, j, nkb, npair = info(f)
                pp = f % 2
                for half in range(2):
                    score(sl, pr, 0, qi, nkb - 1 - (2 * j + half), ZP[pp][half])

            def stB(f):
                pp = f % 2
                zkeys = ["ps%d" % ZP[pp][0], "ps%d" % ZP[pp][1]]
                P.op("act", lambda e: e.activation(out=e32[pp], in_=zpair[pp], func=AF.Exp), reads=zkeys, writes=[("e32", pp)])
                P.op("act", lambda e: e.activation(out=spb[f % 3], in_=e32[pp], func=AF.Ln, bias=1.0, scale=1.0),
                     reads=[("e32", pp)], writes=[("sp", f % 3)])

            def stC(f):
                qi, j, nkb, npair = info(f)
                pp = f % 2
                zkeys = ["ps%d" % ZP[pp][0], "ps%d" % ZP[pp][1]]
                P.op("dve", lambda e: e.tensor_tensor(out=u[pp], in0=zpair[pp], in1=spb[f % 3], op=ALU.subtract),
                     reads=zkeys + [("sp", f % 3)], writes=[("u", pp)])
                s3 = spb[f % 3]
                if j == 0:
                    if j + 1 < npair:
                        P.op("pool", lambda e: e.tensor_tensor(out=accb[(f + 1) % 3], in0=s3[:, 0:TT], in1=s3[:, TT:2 * TT], op=ALU.add),
                             reads=[("sp", f % 3)], writes=[("acc", (f + 1) % 3)])
                else:
                    P.op("pool", lambda e: e.tensor_tensor(out=accm[f % 2], in0=accb[f % 3], in1=s3[:, 0:TT], op=ALU.add),
                         reads=[("acc", f % 3), ("sp", f % 3)], writes=[("accm", f % 2)])
                    if j + 1 < npair:
                        P.op("pool", lambda e: e.tensor_tensor(out=accb[(f + 1) % 3], in0=accm[f % 2], in1=s3[:, TT:2 * TT], op=ALU.add),
                             reads=[("accm", f % 2), ("sp", f % 3)], writes=[("acc", (f + 1) % 3)])

            def stD(f):
                qi, j, nkb, npair = info(f)
                s3 = spb[f % 3]
                rd = [("sp", f % 3)] + ([("acc", f % 3), ("accm", f % 2)] if j > 0 else [])
                P.op("pe", lambda e: e.matmul(ps[TB[0]], lhsT=self.tri_bf, rhs=s3[:, 0:TT], start=True, stop=(j == 0)),
                     reads=rd, writes=["ps%d" % TB[0]], inc=False)
                if j > 0:
                    P.op("pe", lambda e: e.matmul(ps[TB[0]], lhsT=self.ones_bf, rhs=accb[f % 3], start=False, stop=True),
                         reads=rd, writes=["ps%d" % TB[0]], inc=False)
                P.op("pe", lambda e: e.matmul(ps[TB[1]], lhsT=self.tri_bf, rhs=s3[:, TT:2 * TT], start=True, stop=False),
                     reads=rd, writes=["ps%d" % TB[1]], inc=False)
                second = s3[:, 0:TT] if j == 0 else accm[f % 2]
                P.op("pe", lambda e: e.matmul(ps[TB[1]], lhsT=self.ones_bf, rhs=second, start=False, stop=True),
                     reads=rd, writes=["ps%d" % TB[1]], inc=True)

            def stE(f):
                pp = f % 2
                P.op("dve", lambda e: e.tensor_tensor(out=w[pp], in0=u[pp], in1=tailp, op=ALU.subtract),
                     reads=[("u", pp), "ps%d" % TB[0], "ps%d" % TB[1]], writes=[("w", pp)])

            def stF(f):
                pp = f % 2
                P.op("act", lambda e: e.activation(out=A2[pp], in_=w[pp], func=AF.Exp), reads=[("w", pp)], writes=[("A", pp)])

            def stG(f):
                qi, j, nkb, npair = info(f)
                pp = f % 2
                for half in range(2):
                    i = 2 * j + half
                    kbi = nkb - 1 - i
                    P.op("pe", lambda e, i=i, kbi=kbi, half=half: e.matmul(ps[2][0:64, :], lhsT=vaug[sl][:, kbi, 0:64], rhs=A2[pp][:, half * TT:(half + 1) * TT],
                                                                          start=(i == 0), stop=(i == nkb - 1)),
                         reads=[("A", pp), ("hv", sl)], writes=["ps2"], inc=(i == nkb - 1))
                if j == npair - 1:
                    t0 = qi * TT
                    os_ = ostate["n"] % 2
                    ostate["n"] += 1
                    P.op("act", lambda e: e.activation(out=obuf[os_][0:64, :], in_=ps[2][0:64, :], func=AF.Copy), reads=["ps2"], writes=[("ob", os_)])
                    P.dma("sp", ("sto", os_), self.OT[h * 64:(h + 1) * 64, t0:t0 + TT], obuf[os_][0:64, :], reads=[("ob", os_)], writes=[("otd", h, qi)])

            for step in range(F + 2):
                if step < F:
                    stA(step)
                if 1 <= step <= F:
                    stD(step - 1); stE(step - 1)
                if step < F:
                    stB(step)
                if 1 <= step <= F:
                    stF(step - 1)
                if step < F:
                    stC(step)
                if step >= 2:
                    stG(step - 2)

        def fox_head(sl, h):
            pr = slice(0, 70)
            items = [(qi, j) for qi in range(NT) for j in range((4 * qi + 4) // 2)]
            F = len(items)
            ZP3 = [(0, 1), (6, 7), (4, 5)]
            zp3 = [psp[0], psp[3], psp[2]]
            A3 = [A2[0], A2[1], A2x]
            for step in range(F + 2):
                if step < F:
                    qi, jp = items[step]
                    pp = step % 3
                    for half in range(2):
                        score(sl, pr, 1, qi, 2 * jp + half, ZP3[pp][half])
                if step >= 2:
                    f = step - 2
                    qi, j = items[f]
                    nkb = 4 * qi + 4
                    pp = f % 3
                    zkeys = ["ps%d" % ZP3[pp][0], "ps%d" % ZP3[pp][1]]
                    P.op("act", lambda e, pp=pp: e.activation(out=A3[pp], in_=zp3[pp], func=AF.Exp), reads=zkeys, writes=[("A", pp)])
                    for half in range(2):
                        blk = 2 * j + half
                        P.op("pe", lambda e, blk=blk, pp=pp, half=half, nkb=nkb: e.matmul(ps[2], lhsT=vaug[sl][:, blk, :], rhs=A3[pp][:, half * TT:(half + 1) * TT],
                                                                                        start=(blk == 0), stop=(blk == nkb - 1)),
                             reads=[("A", pp), ("hv", sl)], writes=["ps2"], inc=(blk == nkb - 1))
                    if j == nkb // 2 - 1:
                        fox_epi(h, qi)

        def fox_epi(h, qi):
            t0 = qi * TT
            os_ = ostate["n"] % 2
            ostate["n"] += 1
            c0 = cpy[os_][0]
            P.op("act", lambda e: e.activation(out=c0, in_=ps[2], func=AF.Copy), reads=["ps2"], writes=[("cp", os_, 0)])
            P.op("dve", lambda e: e.tensor_copy(out=rinv[0:64, :], in_=c0[64:128, :]), reads=[("cp", os_, 0)], writes=["rinv"])
            P.op("dve", lambda e: e.reciprocal(out=rinv[0:64, :], in_=rinv[0:64, :]), reads=["rinv"], writes=["rinv"])
            P.op("dve", lambda e: e.tensor_tensor(out=obuf[os_][0:64, :], in0=c0[0:64, :], in1=rinv[0:64, :], op=ALU.mult),
                 reads=[("cp", os_, 0), "rinv"], writes=[("ob", os_)])
            r = 256 + h * 64
            P.dma("sp", ("sto", os_), self.OT[r:r + 64, t0:t0 + TT], obuf[os_][0:64, :], reads=[("ob", os_)], writes=[("otd", 4 + h, qi)])

        deferred = []

        def tick():
            for d_ in deferred:
                d_[0] -= 1
            while deferred and deferred[0][0] <= 0:
                deferred.pop(0)[1]()

        def flush():
            while deferred:
                deferred.pop(0)[1]()

        def diff_head(sl, h):
            items = [(qi, j) for qi in range(NT) for j in range((4 * qi + 4) // 2)]
            F = len(items)
            for step in range(F + 1):
                tick()
                if step < F:
                    qi, jp = items[step]
                    for half in range(2):
                        for sub in range(2):
                            score(sl, slice(64 * sub, 64 * sub + 64), 2, qi, 2 * jp + half, ZP[sub][half])
                    for sub in range(2):
                        zkeys = ["ps%d" % ZP[sub][0], "ps%d" % ZP[sub][1]]
                        P.op("act", lambda e, sub=sub, step=step: e.activation(out=A4[sub][step % 2], in_=zpair[sub], func=AF.Exp),
                             reads=zkeys, writes=[("A4", sub, step % 2)])
                if step >= 1:
                    f = step - 1
                    qi, j = items[f]
                    nkb = 4 * qi + 4
                    for sub in range(2):
                        ub, rb = 2 + 2 * sub, 3 + 2 * sub
                        for half in range(2):
                            blk = 2 * j + half
                            rhs = A4[sub][f % 2][:, half * TT:(half + 1) * TT]
                            P.op("pe", lambda e, ub=ub, blk=blk, rhs=rhs, nkb=nkb: e.matmul(ps[ub], lhsT=vdf[sl][:, blk, :], rhs=rhs, start=(blk == 0), stop=(blk == nkb - 1)),
                                 reads=[("A4", sub, f % 2), ("hvd", sl)], writes=["ps%d" % ub], inc=False)
                            P.op("pe", lambda e, rb=rb, blk=blk, rhs=rhs, nkb=nkb: e.matmul(ps[rb], lhsT=self.ones_bf, rhs=rhs, start=(blk == 0), stop=(blk == nkb - 1)),
                                 reads=[("A4", sub, f % 2)], writes=["ps%d" % rb], inc=(half == 1))
                    if j == nkb // 2 - 1:
                        diff_epi(h, qi)

        def diff_epi(h, qi):
            t0 = qi * TT
            os_ = ostate["n"] % 2
            ostate["n"] += 1
            cU1, cR1, cU2, cR2 = cpy[os_]
            P.op("act", lambda e: e.activation(out=cU1, in_=ps[2], func=AF.Copy), reads=["ps2"], writes=[("cp", os_, 0)])
            P.op("dve", lambda e: e.tensor_copy(out=cR1, in_=ps[3]), reads=["ps3"], writes=[("cp", os_, 1)])
            P.op("act", lambda e: e.activation(out=cU2, in_=ps[4], func=AF.Copy), reads=["ps4"], writes=[("cp", os_, 2)])
            P.op("dve", lambda e: e.tensor_copy(out=cR2, in_=ps[5]), reads=["ps5"], writes=[("cp", os_, 3)])

            def part2():
                P.op("dve", lambda e: e.reciprocal(out=rinv, in_=cR1), reads=[("cp", os_, 1)], writes=["rinv"])
                P.op("dve", lambda e: e.tensor_tensor(out=a1, in0=cU1, in1=rinv, op=ALU.mult), reads=[("cp", os_, 0), "rinv"], writes=["a1"])
                P.op("dve", lambda e: e.reciprocal(out=rinv, in_=cR2), reads=[("cp", os_, 3), "a1"], writes=["rinv"])
                P.op("dve", lambda e: e.tensor_tensor(out=a2, in0=cU2, in1=rinv, op=ALU.mult), reads=[("cp", os_, 2), "rinv"], writes=["a2"])
                P.op("dve", lambda e: e.scalar_tensor_tensor(out=oc, in0=a2, scalar=neglam, in1=a1, op0=ALU.mult, op1=ALU.add),
                     reads=["a1", "a2", "neglam"], writes=["oc"])
                P.op("act", lambda e: e.activation(out=sqo, in_=oc, func=AF.Square), reads=["oc"], writes=["sqo"])
                P.op("pe", lambda e: e.matmul(ps[6], lhsT=self.ones_bf, rhs=sqo, start=True, stop=True), reads=["sqo"], writes=["ps6"])
                P.op("act", lambda e: e.activation(out=rso, in_=ps[6], func=AF.Ln, bias=self.eps_col, scale=1.0 / 128), reads=["ps6"], writes=["rso"])
                P.op("act", lambda e: e.activation(out=rso, in_=rso, func=AF.Exp, scale=-0.5), reads=["rso"], writes=["rso"])
                P.op("dve", lambda e: e.scalar_tensor_tensor(out=obuf[os_], in0=oc, scalar=gsub, in1=rso, op0=ALU.mult, op1=ALU.mult),
                     reads=["oc", "rso", "gsub"], writes=[("ob", os_)])
                r = 512 + h * 128
                P.dma("sp", ("sto", os_), self.OT[r:r + 128, t0:t0 + TT], obuf[os_], reads=[("ob", os_)], writes=[("otd", 8 + h, qi)])
            deferred.append([4, part2])

        hl = self.head_list if getattr(self, "head_list", None) is not None else list(range(12))
        heads = [heads[i] for i in hl]
        load_head(0)
        for n, (typ, h) in enumerate(heads):
            if n + 1 < len(heads):
                load_head(n + 1)
            sl = n % 2
            if typ == "sb":
                sb_head(sl, h)
            elif typ == "fox":
                fox_head(sl, h)
            else:
                diff_head(sl, h)
            flush()


    def stage_merge(self, l, dst):
        P = self.P
        self.new_stage()
        ps = self.ps
        w_in = self.w["w_in"][l].rearrange("(kc p) n -> p kc n", p=128)
        wg = self.carve([128, 8, 3072], BF16)
        wbr = self.carve([128, 8, D], BF16)
        wo = self.carve([128, 8, D], BF16)
        NS = 3
        hts = [self.carve([128, 8, TT], F32) for _ in range(NS)]
        xns = [self.carve([128, 8, TT], BF16) for _ in range(2)]
        sq = self.carve([128, 8, TT], BF16)
        rs = self.carve([128, TT], F32)
        ots = [self.carve([128, 8, TT], BF16) for _ in range(2)]
        sg = [self.carve([128, TT], F32) for _ in range(3)]
        m1 = self.carve([128, TT], F32)
        m2 = self.carve([128, TT], F32)
        m3 = self.carve([128, TT], F32)
        mg = self.carve([128, 8, TT], BF16)
        P.dma_many("pool", ("w", "wg"), [(wg[:, kc, :], w_in[:, kc, 3076:6148]) for kc in range(8)], writes=["wg"])
        self.load_w(wbr, self.w["w_br"][l], "wbr", 8)
        self.load_w(wo, self.w["w_o"][l], "wo", 8)
        OTv = self.OT.rearrange("(c p) t -> p c t", p=128)
        rcs = [(0, 2), (2, 4), (4, 8)]

        def load_ot(slot, ti):
            P.dma("sp", ("ldot", slot), ots[slot], OTv[:, :, ti * TT:(ti + 1) * TT], writes=[("ot", slot)])
        for t_ in range(min(NS - 1, self.NT)):
            self.load_h(t_ % NS, t_, hts[t_ % NS])
        load_ot(0, 0)
        for ti in range(self.NT):
            slot = ti % NS
            x2 = ti % 2
            ht, ot = hts[slot], ots[x2]
            hkey = ("ht", slot)
            nslot = (ti + 1) % NS
            if ti + NS - 1 < self.NT:
                self.load_h((ti + NS - 1) % NS, ti + NS - 1, hts[(ti + NS - 1) % NS])
            if ti + 1 < self.NT:
                load_ot(1 - x2, ti + 1)
            xn, xkey = xns[x2], ("xn", x2)
            if ti == 0:
                self.norm(ht, hkey, 8, xn, xkey, sq, rs)
            for dc in range(8):
                if dc == 2 and ti + 1 < self.NT:
                    self.norm_a(hts[nslot], ("ht", nslot), sq)
                if dc == 4 and ti + 1 < self.NT:
                    self.norm_b(hts[nslot], ("ht", nslot), 8, xns[1 - x2], ("xn", 1 - x2), sq, rs)
                dsl = slice(dc * 128, (dc + 1) * 128)
                for bi in range(3):
                    for kc in range(8):
                        P.op("pe", lambda e, bi=bi, kc=kc, dc=dc, xn=xn: e.matmul(ps[3 + bi], lhsT=wg[:, kc, bi * 1024 + dc * 128:bi * 1024 + (dc + 1) * 128],
                                                                          rhs=xn[:, kc, :], start=(kc == 0), stop=(kc == 7)),
                             reads=["wg", xkey], writes=["ps%d" % (3 + bi)], inc=(kc == 7))
                for bi, (r0, r1) in enumerate(rcs):
                    for rc in range(r0, r1):
                        P.op("pe", lambda e, bi=bi, rc=rc, dsl=dsl, r0=r0, r1=r1, ot=ot: e.matmul(ps[bi], lhsT=wbr[:, rc, dsl], rhs=ot[:, rc, :],
                                                                                          start=(rc == r0), stop=(rc == r1 - 1)),
                             reads=["wbr", ("ot", x2)], writes=["ps%d" % bi], inc=(rc == r1 - 1))
                for bi in range(3):
                    P.op("act", lambda e, bi=bi: e.activation(out=sg[bi], in_=ps[3 + bi], func=AF.Sigmoid), reads=["ps%d" % (3 + bi)], writes=[("sg", bi)])
                P.op("dve", lambda e: e.tensor_tensor(out=m1, in0=ps[0], in1=sg[0], op=ALU.mult), reads=["ps0", ("sg", 0)], writes=["m1"])
                P.op("dve", lambda e: e.tensor_tensor(out=m2, in0=ps[1], in1=sg[1], op=ALU.mult), reads=["ps1", ("sg", 1)], writes=["m2"])
                P.op("dve", lambda e: e.tensor_tensor(out=m3, in0=ps[2], in1=sg[2], op=ALU.mult), reads=["ps2", ("sg", 2)], writes=["m3"])
                P.op("pool", lambda e: e.tensor_tensor(out=m1, in0=m1, in1=m2, op=ALU.add), reads=["m1", "m2"], writes=["m1"])
                P.op("pool", lambda e, dc=dc: e.tensor_tensor(out=mg[:, dc, :], in0=m1, in1=m3, op=ALU.add), reads=["m1", "m3"], writes=["mg"])
            for dc in range(8):
                dsl = slice(dc * 128, (dc + 1) * 128)
                pb = 6 + dc % 2
                pk = "ps%d" % pb
                for mc in range(8):
                    P.op("pe", lambda e, mc=mc, dsl=dsl, pb=pb: e.matmul(ps[pb], lhsT=wo[:, mc, dsl], rhs=mg[:, mc, :], start=(mc == 0), stop=(mc == 7)),
                         reads=["wo", "mg"], writes=[pk], inc=(mc == 7))
                P.op("dve", lambda e, dc=dc, ht=ht, pb=pb: e.tensor_tensor(out=ht[:, dc, :], in0=ps[pb], in1=ht[:, dc, :], op=ALU.add),
                     reads=[pk, hkey], writes=[hkey])
            self.store_h(slot, ti, ht, dst)
        self.h_cur = dst

    def stage_ple(self, l, dst):
        P = self.P
        self.new_stage()
        ps = self.ps
        pg = self.carve([128, 8, D], BF16)
        pp = self.carve([128, 2, D], BF16)
        NS = 4
        hts = [self.carve([128, 8, TT], F32) for _ in range(NS)]
        xns = [self.carve([128, 8, TT], BF16) for _ in range(2)]
        sq = self.carve([128, 8, TT], BF16)
        rs = self.carve([128, TT], F32)
        pts = [self.carve([128, 2, TT], BF16) for _ in range(2)]
        sg = [self.carve([128, TT], F32) for _ in range(2)]
        tm = [self.carve([128, TT], F32) for _ in range(2)]
        self.load_w(pg, self.w["ple_gate_w"][l], "pg", 8)
        self.load_w(pp, self.w["ple_proj_w"][l], "pp", 2)
        pTv = self.pT[l].rearrange("(c p) t -> p c t", p=128)

        def load_pt(slot, ti):
            P.dma("pool", ("ldpt", slot), pts[slot], pTv[:, :, ti * TT:(ti + 1) * TT], writes=[("pt", slot)])
        for t_ in range(min(NS - 1, self.NT)):
            self.load_h(t_ % NS, t_, hts[t_ % NS])
        load_pt(0, 0)
        for ti in range(self.NT):
            slot = ti % NS
            x2 = ti % 2
            ht, pt = hts[slot], pts[x2]
            hkey = ("ht", slot)
            nslot = (ti + 1) % NS
            if ti + NS - 1 < self.NT:
                self.load_h((ti + NS - 1) % NS, ti + NS - 1, hts[(ti + NS - 1) % NS])
            if ti + 1 < self.NT:
                load_pt(1 - x2, ti + 1)
            xn, xkey = xns[x2], ("xn", x2)
            if ti == 0:
                self.norm(ht, hkey, 24, xn, xkey, sq, rs)
            for dc in range(8):
                if dc == 1 and ti + 1 < self.NT:
                    self.norm_a(hts[nslot], ("ht", nslot), sq)
                if dc == 4 and ti + 1 < self.NT:
                    self.norm_b(hts[nslot], ("ht", nslot), 24, xns[1 - x2], ("xn", 1 - x2), sq, rs)
                dsl = slice(dc * 128, (dc + 1) * 128)
                b2 = dc % 2
                for kc in range(8):
                    P.op("pe", lambda e, kc=kc, dsl=dsl, b2=b2, xn=xn: e.matmul(ps[b2], lhsT=pg[:, kc, dsl], rhs=xn[:, kc, :], start=(kc == 0), stop=(kc == 7)),
                         reads=["pg", xkey], writes=["ps%d" % b2], inc=(kc == 7))
                for kc in range(2):
                    P.op("pe", lambda e, kc=kc, dsl=dsl, b2=b2, pt=pt: e.matmul(ps[2 + b2], lhsT=pp[:, kc, dsl], rhs=pt[:, kc, :], start=(kc == 0), stop=(kc == 1)),
                         reads=["pp", ("pt", x2)], writes=["ps%d" % (2 + b2)], inc=(kc == 1))
                P.op("act", lambda e, b2=b2: e.activation(out=sg[b2], in_=ps[b2], func=AF.Sigmoid), reads=["ps%d" % b2], writes=[("sg", b2)])
                P.op("dve", lambda e, b2=b2: e.tensor_tensor(out=tm[b2], in0=ps[2 + b2], in1=sg[b2], op=ALU.mult),
                     reads=["ps%d" % (2 + b2), ("sg", b2)], writes=[("tm", b2)])
                P.op("pool", lambda e, b2=b2, dc=dc, ht=ht: e.tensor_tensor(out=ht[:, dc, :], in0=ht[:, dc, :], in1=tm[b2], op=ALU.add),
                     reads=[("tm", b2), hkey], writes=[hkey])
            self.store_h(slot, ti, ht, dst)
        self.h_cur = dst

    def build(self):
        nc = self.nc
        with ExitStack() as st:
            self.P = P = Prog(nc, st)
            self.BIG32 = 52448
            self.big = st.enter_context(nc.sbuf_tensor("big", [128, self.BIG32], F32))[:]
            cbf_t = st.enter_context(nc.sbuf_tensor("cbf_t", [128, 5 * 128], BF16))
            c32_t = st.enter_context(nc.sbuf_tensor("c32_t", [128, 2 * 128 + 3], F32))
            cols_t = st.enter_context(nc.sbuf_tensor("cols_t", [128, 4 * NCOLS], F32))
            self.psp = [st.enter_context(nc.psum_tensor("psp%d" % i, [128, 1024], F32))[:] for i in range(4)]
            self.ps = [self.psp[i // 2][:, (i % 2) * 512:(i % 2 + 1) * 512] for i in range(8)]
            self.off = 0
            self.ones_bf = cbf_t[:, 0:128]
            self.bd_bf = cbf_t[:, 128:256]
            self.id_bf = cbf_t[:, 256:384]
            self.pm_bf = cbf_t[:, 384:512]
            self.tri_bf = cbf_t[:, 512:640]
            self.tri32 = c32_t[:, 0:128]
            self.ones32 = c32_t[:, 128:256]
            self.eps_col = c32_t[:, 256:257]
            self.invf_col = c32_t[:, 257:258]
            self.negpi_col = c32_t[:, 258:259]
            self.cols_all = cols_t
            P.dma("pool", ("c", 0), cbf_t[:], self.cbf_d[:, 0:640], writes=["cbf"])
            P.dma("sp", ("c", 1), c32_t[:], self.c32_d, writes=["c32"])
            L = self.cols_d.shape[0]
            P.dma_many("sp", ("c", 2), [(cols_t[:, i * NCOLS:(i + 1) * NCOLS], self.cols_d[i]) for i in range(L)], writes=["cols"])
            P.barrier()
            self.h_cur = self.xT
            nl = len(self.layers)
            if self.stages is None:
                self.stage_setup()
            for li, l in enumerate(self.layers):
                self.cols = cols_t[:, l * NCOLS:(l + 1) * NCOLS]
                last = (li == nl - 1)
                stages = self.stages or ["ffn1", "proj", "attn", "merge", "ffn2", "ple"]
                for sname in stages:
                    is_last_stage = last and sname == stages[-1]
                    dst = self.yT if is_last_stage else self.hA
                    if sname == "setup":
                        self.stage_setup()
                    elif sname == "proj":
                        self.stage_proj(l)
                    elif sname == "attn":
                        self.stage_attn(l)
                    elif sname == "merge":
                        self.stage_merge(l, dst)
                    elif sname == "ple":
                        self.stage_ple(l, dst)
                    elif sname == "ffn1":
                        self.stage_ffn(l, self.w["ffn1_wi"], self.w["ffn1_wo"], 0, dst)
                    elif sname == "ffn2":
                        self.stage_ffn(l, self.w["ffn2_wi"], self.w["ffn2_wo"], 16, dst)
            P.barrier()
            P.emit()
        return nc


def make_consts():
    cbf = np.zeros((128, 5 * 128 + 12 * 512), np.float32)
    p = np.arange(128)
    cbf[:, 0:128] = 1.0
    cbf[:, 128:256] = (p[:, None] // 64 == p[None, :] // 64).astype(np.float32)
    cbf[:, 256:384] = np.eye(128, dtype=np.float32)
    pm = np.zeros((128, 128), np.float32)
    for m in range(128):
        r = m % 64
        if r < 8:
            pm[m + 8, m] = -1.0
        elif r < 16:
            pm[m - 8, m] = 1.0
    cbf[:, 384:512] = pm
    cbf[:, 512:640] = (p[:, None] > p[None, :]).astype(np.float32)
    j = np.arange(512)
    for typ in range(3):
        for m in range(4):
            s = p[:, None] + 128 * m
            if typ == 0:
                keep = s < j[None, :]
            elif typ == 1:
                keep = s <= j[None, :]
            else:
                keep = (s // 64) <= (j[None, :] // 64)
            cbf[:, 640 + (typ * 4 + m) * 512: 640 + (typ * 4 + m + 1) * 512] = np.where(keep, 0.0, NEG)
    c32 = np.zeros((128, 2 * 128 + 3), np.float32)
    c32[:, 258] = -math.pi
    c32[:, 0:128] = (p[:, None] > p[None, :]).astype(np.float32)
    c32[:, 128:256] = 1.0
    c32[:, 256] = EPS
    inv_freq = (500000.0 ** (-np.arange(0, 16, 2, dtype=np.float32) / np.float32(16))).astype(np.float32)
    r = p % 64
    c32[:, 257] = np.where(r < 16, inv_freq[r % 8], 0.0)
    return cbf, c32


def make_cols(inp, L):
    cols = np.zeros((L, 128, NCOLS), np.float32)
    for l in range(L):
        cols[l, :, 0:8] = np.asarray(inp["ffn1_norm"][l]).reshape(8, 128).T
        cols[l, :, 8:16] = np.asarray(inp["mix_norm"][l]).reshape(8, 128).T
        cols[l, :, 16:24] = np.asarray(inp["ffn2_norm"][l]).reshape(8, 128).T
        cols[l, :, 24:32] = np.asarray(inp["ple_norm"][l]).reshape(8, 128).T
        cols[l, :, 32] = np.tile(np.asarray(inp["qk_gain_fox"][l][0]), 2)
        cols[l, :, 33] = np.tile(np.asarray(inp["qk_gain_fox"][l][1]), 2)
        cols[l, :, 34] = np.tile(np.asarray(inp["qk_gain_diff"][l][0]), 2)
        cols[l, :, 35] = np.tile(np.asarray(inp["qk_gain_diff"][l][1]), 2)
        cols[l, :, 36] = np.asarray(inp["diff_subln"][l])
        cols[l, 0:4, 37] = np.asarray(inp["b_forget"][l])
    return cols


LAM_INITS = [0.8 - 0.6 * math.exp(-0.3 * i) for i in range(4)]


def prep_shared(inp):
    L = 4
    f = lambda a: np.ascontiguousarray(np.asarray(a), dtype=np.float32)
    cbf, c32 = make_consts()
    sh = {nm: f(inp[nm]) for nm in ["ffn1_wi", "ffn1_wo", "w_in", "w_br", "w_o", "ffn2_wi", "ffn2_wo", "ple_gate_w", "ple_proj_w"]}
    sh["cols"] = make_cols(inp, L)
    sh["dlam"] = np.ascontiguousarray(np.broadcast_to(np.asarray(inp["diff_lambda"], np.float32).reshape(L, 1, 256), (L, 128, 256)))
    sh["cbf"] = cbf
    sh["c32"] = c32
    return sh


def prep_core(inp, b):
    x = np.asarray(inp["x"][b], np.float32)
    p = np.asarray(inp["p"][:, b], np.float32)
    pos = np.asarray(inp["positions"][b], np.int32)
    S = x.shape[0]
    return {
        "xT": np.ascontiguousarray(x.T),
        "pT": np.ascontiguousarray(p.transpose(0, 2, 1)),
        "posb": np.ascontiguousarray(np.broadcast_to(pos[None, :], (128, S))),
    }


_NC_CACHE = {}


def kernel(**inputs):
    B, S, _ = inputs["x"].shape
    key = (S,)
    if key not in _NC_CACHE:
        _NC_CACHE[key] = Builder(S, [0, 1, 2, 3], 4, LAM_INITS).build()
    nc = _NC_CACHE[key]
    sh = prep_shared(inputs)
    in_maps = []
    for b in range(B):
        m = dict(sh)
        m.update(prep_core(inputs, b))
        in_maps.append(m)
    res = run_bass_kernel_spmd(nc, in_maps, core_ids=list(range(B)))
    out = np.stack([np.ascontiguousarray(r["yT"].T) for r in res.results], axis=0)
    return out.astype(np.float32)
```
